# Optimizing a Trainium2 kernel written in Bass

```python
import jax, jax.numpy as jnp
from jax import lax
import numpy as np

D_MODEL = 1024
BATCH = 8
SEQ = 4096
DEPTH = 2

N_MIXERS = 2
EPS = 1e-6

N_MEM = 256
MEM_HEADS = 4
MEM_HEAD_DIM = 64
MEM_WIDTH = MEM_HEADS * MEM_HEAD_DIM
MIX_WIDTH = D_MODEL - MEM_WIDTH

ML_HEADS = 4
ML_V_DIM = MIX_WIDTH // ML_HEADS
ML_QK_DIM = ML_V_DIM // 2
ML_CONV = 4
ML_CHUNK = 128
ML_IN_WIDTH = 2 * ML_HEADS * ML_QK_DIM + 2 * MIX_WIDTH + 2 * ML_HEADS + MEM_WIDTH

MLA_HEADS = 12
MLA_NOPE = 64
MLA_ROPE = 32
MLA_V = MIX_WIDTH // MLA_HEADS
MLA_Q_RANK = 384
MLA_KV_RANK = 256
MLA_IN_WIDTH = MLA_Q_RANK + MLA_KV_RANK + MLA_ROPE + MEM_WIDTH
ROPE_THETA = 10000.0
Q_BLOCK = 128

D_FF = 4 * D_MODEL

kernel_name = 'hybrid_mlstm_mla_memory_trunk'

F32 = jnp.float32


def rms_norm(x, g):
    xf = x.astype(F32)
    y = xf * lax.rsqrt(jnp.mean(xf * xf, axis=-1, keepdims=True) + EPS)
    return (y * g.astype(F32)).astype(x.dtype)


def squared_relu_mlp(h, w1, w2):
    return jnp.square(jax.nn.relu(h @ w1)) @ w2


def memory_cross_attention(q_mem, mem_k, mem_v):
    B, S, _ = q_mem.shape
    q = q_mem.reshape(B, S, MEM_HEADS, MEM_HEAD_DIM)
    s = jnp.einsum('bqhd,bmhd->bhqm', q, mem_k).astype(F32) * (MEM_HEAD_DIM ** -0.5)
    p = jax.nn.softmax(s, axis=-1).astype(mem_v.dtype)
    o = jnp.einsum('bhqm,bmhd->bqhd', p, mem_v)
    return o.reshape(B, S, MEM_WIDTH)


def causal_short_conv(x, w):
    K = w.shape[0]
    S = x.shape[1]
    xp = jnp.pad(x, ((0, 0), (K - 1, 0), (0, 0)))
    return sum(xp[:, j:j + S] * w[j] for j in range(K))


def mlstm_chunkwise(q, k, v, log_i, log_f):
    B, S, H, dk = q.shape
    dv = v.shape[-1]
    L = ML_CHUNK
    nc = S // L

    def chunks(t):
        t = t.astype(F32).reshape((B, nc, L) + t.shape[2:])
        return jnp.moveaxis(t, 1, 0).swapaxes(2, 3)

    causal = jnp.tril(jnp.ones((L, L), dtype=bool))

    def step(carry, inp):
        C, n, m = carry
        qc, kc, vc, ic, fc = inp
        b = jnp.cumsum(fc, axis=-1)
        log_w = jnp.where(causal, b[..., :, None] - b[..., None, :] + ic[..., None, :], -jnp.inf)
        log_inter = b + m[..., None]
        m_t = jnp.maximum(log_inter, jnp.max(log_w, axis=-1))
        w = jnp.exp(log_w - m_t[..., None])
        a_inter = jnp.exp(log_inter - m_t)
        s = jnp.einsum('bhjd,bhsd->bhjs', qc, kc) * w
        num = jnp.einsum('bhjs,bhsv->bhjv', s, vc) + a_inter[..., None] * jnp.einsum('bhjd,bhdv->bhjv', qc, C)
        den = jnp.sum(s, axis=-1) + a_inter * jnp.einsum('bhjd,bhd->bhj', qc, n)
        h = num / jnp.maximum(jnp.abs(den), jnp.exp(-m_t))[..., None]
        b_end = b[..., -1]
        log_u = b_end[..., None] - b + ic
        m_new = jnp.maximum(b_end + m, jnp.max(log_u, axis=-1))
        u = jnp.exp(log_u - m_new[..., None])
        decay = jnp.exp(b_end + m - m_new)
        uk = kc * u[..., None]
        C = decay[..., None, None] * C + jnp.einsum('bhsd,bhsv->bhdv', uk, vc)
        n = decay[..., None] * n + jnp.sum(uk, axis=2)
        return (C, n, m_new), h

    init = (jnp.zeros((B, H, dk, dv), F32), jnp.zeros((B, H, dk), F32), jnp.zeros((B, H), F32))
    _, hs = lax.scan(step, init, (chunks(q), chunks(k), chunks(v), chunks(log_i), chunks(log_f)))
    return jnp.moveaxis(hs.swapaxes(2, 3), 0, 1).reshape(B, S, H, dv)


def mlstm_mixer(h, mem_k, mem_v, w_in, b_igate, b_fgate, w_conv, w_hnorm, w_out):
    B, S, _ = h.shape
    qk_w = 2 * ML_HEADS * ML_QK_DIM
    cuts = [qk_w, qk_w + MIX_WIDTH, qk_w + 2 * MIX_WIDTH,
            qk_w + 2 * MIX_WIDTH + ML_HEADS, qk_w + 2 * MIX_WIDTH + 2 * ML_HEADS]
    qk, v, o_pre, i_pre, f_pre, q_mem = jnp.split(h @ w_in, cuts, axis=-1)
    qk = jax.nn.silu(causal_short_conv(qk, w_conv))
    q, k = jnp.split(qk, 2, axis=-1)
    q = q.reshape(B, S, ML_HEADS, ML_QK_DIM)
    k = k.reshape(B, S, ML_HEADS, ML_QK_DIM) * (ML_QK_DIM ** -0.5)
    v = v.reshape(B, S, ML_HEADS, ML_V_DIM)
    log_i = (i_pre + b_igate).astype(F32)
    log_f = jax.nn.log_sigmoid((f_pre + b_fgate).astype(F32))
    h_cell = mlstm_chunkwise(q, k, v, log_i, log_f).astype(h.dtype)
    h_cell = rms_norm(h_cell, w_hnorm)
    y_ml = h_cell.reshape(B, S, MIX_WIDTH) * jax.nn.sigmoid(o_pre)
    y_mem = memory_cross_attention(q_mem, mem_k, mem_v)
    return jnp.concatenate([y_ml, y_mem], axis=-1) @ w_out


def rope_cos_sin(positions):
    inv = ROPE_THETA ** (-jnp.arange(0, MLA_ROPE, 2, dtype=F32) / MLA_ROPE)
    ang = positions.astype(F32)[..., None] * inv
    return jnp.cos(ang), jnp.sin(ang)


def apply_rope(x, cos, sin):
    xf = x.astype(F32)
    x1, x2 = jnp.split(xf, 2, axis=-1)
    return jnp.concatenate([x1 * cos - x2 * sin, x2 * cos + x1 * sin], axis=-1).astype(x.dtype)


def mla_causal_attention(q_nope, q_rope, k_nope, k_rope, v):
    B, S, H, _ = q_nope.shape
    nb = S // Q_BLOCK
    scale = (MLA_NOPE + MLA_ROPE) ** -0.5
    k_pos = jnp.arange(S)

    def block(i):
        start = i * Q_BLOCK
        qn = lax.dynamic_slice_in_dim(q_nope, start, Q_BLOCK, axis=1)
        qr = lax.dynamic_slice_in_dim(q_rope, start, Q_BLOCK, axis=1)
        s = (jnp.einsum('bqhd,bkhd->bhqk', qn, k_nope)
             + jnp.einsum('bqhd,bkd->bhqk', qr, k_rope)).astype(F32) * scale
        q_pos = start + jnp.arange(Q_BLOCK)
        s = jnp.where(k_pos[None, :] <= q_pos[:, None], s, -jnp.inf)
        p = jax.nn.softmax(s, axis=-1).astype(v.dtype)
        return jnp.einsum('bhqk,bkhd->bqhd', p, v)

    out = lax.map(block, jnp.arange(nb))
    return jnp.moveaxis(out, 0, 1).reshape(B, S, H * MLA_V)


def mla_mixer(h, cos, sin, mem_k, mem_v, w_in, w_qnorm, w_uq, w_kvnorm, w_ukv, w_out):
    B, S, _ = h.shape
    cuts = [MLA_Q_RANK, MLA_Q_RANK + MLA_KV_RANK, MLA_Q_RANK + MLA_KV_RANK + MLA_ROPE]
    c_q, c_kv, k_rope, q_mem = jnp.split(h @ w_in, cuts, axis=-1)
    q = (rms_norm(c_q, w_qnorm) @ w_uq).reshape(B, S, MLA_HEADS, MLA_NOPE + MLA_ROPE)
    q_nope, q_rope = q[..., :MLA_NOPE], q[..., MLA_NOPE:]
    kv = (rms_norm(c_kv, w_kvnorm) @ w_ukv).reshape(B, S, MLA_HEADS, MLA_NOPE + MLA_V)
    k_nope, v = kv[..., :MLA_NOPE], kv[..., MLA_NOPE:]
    q_rope = apply_rope(q_rope, cos[:, :, None, :], sin[:, :, None, :])
    k_rope = apply_rope(k_rope, cos, sin)
    y_mla = mla_causal_attention(q_nope, q_rope, k_nope, k_rope, v)
    y_mem = memory_cross_attention(q_mem, mem_k, mem_v)
    return jnp.concatenate([y_mla, y_mem], axis=-1) @ w_out


def setup_inputs(seed: int = 0) -> dict:
    key = jax.random.key(seed)
    ks = jax.random.split(key, 32)

    def dense(k, fi, fo):
        return jax.random.normal(k, (fi, fo), F32) * (fi ** -0.5)

    def gain(k, *shape):
        return 1.0 + 0.02 * jax.random.normal(k, shape, F32)

    x = jax.random.normal(ks[0], (BATCH, SEQ, D_MODEL), F32)
    mem = jax.random.normal(ks[1], (BATCH, N_MEM, D_MODEL), F32)
    positions = (jax.random.randint(ks[2], (BATCH, 1), 0, 1024, dtype=jnp.int32)
                 + jnp.arange(SEQ, dtype=jnp.int32)[None, :])
    return {
        'x': x,
        'mem': mem,
        'positions': positions,
        'mem_norm': gain(ks[3], D_MODEL),
        'w_mem_kv': dense(ks[4], D_MODEL, 2 * MEM_WIDTH),
        'norm_mix0': gain(ks[5], D_MODEL),
        'w_in0': dense(ks[6], D_MODEL, ML_IN_WIDTH),
        'b_igate0': 0.1 * jax.random.normal(ks[7], (ML_HEADS,), F32),
        'b_fgate0': jnp.linspace(3.0, 6.0, ML_HEADS, dtype=F32) + 0.1 * jax.random.normal(ks[8], (ML_HEADS,), F32),
        'w_conv0': jax.random.normal(ks[9], (ML_CONV, 2 * ML_HEADS * ML_QK_DIM), F32) * (ML_CONV ** -0.5),
        'w_hnorm0': gain(ks[10], ML_HEADS, ML_V_DIM),
        'w_out0': dense(ks[11], D_MODEL, D_MODEL),
        'norm_ffn0': gain(ks[12], D_MODEL),
        'w_ff1_0': dense(ks[13], D_MODEL, D_FF),
        'w_ff2_0': dense(ks[14], D_FF, D_MODEL),
        'norm_mix1': gain(ks[15], D_MODEL),
        'w_in1': dense(ks[16], D_MODEL, MLA_IN_WIDTH),
        'w_qnorm1': gain(ks[17], MLA_Q_RANK),
        'w_uq1': dense(ks[18], MLA_Q_RANK, MLA_HEADS * (MLA_NOPE + MLA_ROPE)),
        'w_kvnorm1': gain(ks[19], MLA_KV_RANK),
        'w_ukv1': dense(ks[20], MLA_KV_RANK, MLA_HEADS * (MLA_NOPE + MLA_V)),
        'w_out1': dense(ks[21], D_MODEL, D_MODEL),
        'norm_ffn1': gain(ks[22], D_MODEL),
        'w_ff1_1': dense(ks[23], D_MODEL, D_FF),
        'w_ff2_1': dense(ks[24], D_FF, D_MODEL),
        'final_norm': gain(ks[25], D_MODEL),
    }


def reference(x, mem, positions, mem_norm, w_mem_kv,
              norm_mix0, w_in0, b_igate0, b_fgate0, w_conv0, w_hnorm0, w_out0,
              norm_ffn0, w_ff1_0, w_ff2_0,
              norm_mix1, w_in1, w_qnorm1, w_uq1, w_kvnorm1, w_ukv1, w_out1,
              norm_ffn1, w_ff1_1, w_ff2_1, final_norm):
    Bm = mem.shape[0]
    mem_k, mem_v = jnp.split(rms_norm(mem, mem_norm) @ w_mem_kv, 2, axis=-1)
    mem_k = mem_k.reshape(Bm, N_MEM, MEM_HEADS, MEM_HEAD_DIM)
    mem_v = mem_v.reshape(Bm, N_MEM, MEM_HEADS, MEM_HEAD_DIM)
    cos, sin = rope_cos_sin(positions)
    ffn_params = [(norm_ffn0, w_ff1_0, w_ff2_0), (norm_ffn1, w_ff1_1, w_ff2_1)]
    for layer in range(DEPTH):
        if layer % N_MIXERS == 0:
            x = x + mlstm_mixer(rms_norm(x, norm_mix0), mem_k, mem_v,
                                w_in0, b_igate0, b_fgate0, w_conv0, w_hnorm0, w_out0)
        else:
            x = x + mla_mixer(rms_norm(x, norm_mix1), cos, sin, mem_k, mem_v,
                              w_in1, w_qnorm1, w_uq1, w_kvnorm1, w_ukv1, w_out1)
        g, w1, w2 = ffn_params[layer]
        x = x + squared_relu_mlp(rms_norm(x, g), w1, w2)
    return rms_norm(x, final_norm)
```

```python
import numpy as np
from contextlib import ExitStack
import concourse.bass as bass
import concourse.mybir as mybir

F32 = mybir.dt.float32
BF16 = mybir.dt.bfloat16
I32 = mybir.dt.int32
AF = mybir.ActivationFunctionType
ALU = mybir.AluOpType
AX = mybir.AxisListType

ENGS = ("tensor", "vector", "scalar", "gpsimd", "sync")
SEM_ROLL = 30000


class Tok:
    __slots__ = ("sem", "val", "dma")

    def __init__(self, sem, val, dma=False):
        self.sem = sem
        self.val = val
        self.dma = dma


class Buf:
    def __init__(self, t, name=""):
        self.t = t
        self.name = name
        self.w = {}
        self.r = {}
        self.dsem = None
        self.is_dram = False
        self.multi = False

    def __getitem__(self, idx):
        return self.t[idx]


class DSem:
    def __init__(self, sem):
        self.sem = sem
        self.total = 0
        self.open = False


class Prog:
    def __init__(self, nc, strict_same=("vector", "scalar", "gpsimd")):
        self.nc = nc
        self.stack = ExitStack()
        self.ops = {e: [] for e in ENGS}
        self.sem = {}
        self.cnt = {}
        self.seen = {e: {} for e in ENGS}
        self.strict = set(strict_same)
        self.nsem = 0
        self.dsems = {}
        for e in ("tensor", "vector", "scalar", "gpsimd"):
            self.sem[e] = self._new_sem("c_" + e)
            self.cnt[e] = 0
        self.nops = 0
        import os
        self.same_waw = os.environ.get('SAME_WAW', '1') == '1'
        self.rec = []
        self.sched = True
        self.nfill = 0

    def _new_sem(self, name):
        self.nsem += 1
        return self.stack.enter_context(self.nc.semaphore(name + "_%d" % self.nsem))

    def sbuf(self, name, shape, dt):
        t = self.stack.enter_context(self.nc.sbuf_tensor(name, list(shape), dt))
        return Buf(t, name)

    def psum(self, name, shape, dt):
        t = self.stack.enter_context(self.nc.psum_tensor(name, list(shape), dt))
        return Buf(t, name)

    def dram(self, name, shape, dt, kind="Internal"):
        t = self.nc.dram_tensor(name, list(shape), dt, kind=kind)
        b = Buf(t.ap(), name)
        b.is_dram = True
        return b

    def view(self, buf, name=""):
        return Buf(buf.t, name or buf.name)

    def _collect(self, eng, reads, writes):
        need = {}
        own = self.sem.get(eng)

        def add(t, raw):
            if t is None:
                return
            if t.sem is own and not raw and not self.same_waw:
                return
            k = id(t.sem)
            if k not in need or need[k].val < t.val:
                need[k] = t

        for b in reads:
            for t in b.w.values():
                add(t, True)
        for b in writes:
            if b.multi:
                continue
            for t in b.w.values():
                add(t, False)
            for t in b.r.values():
                add(t, False)
        out = []
        for t in need.values():
            if t.sem is own and eng not in self.strict:
                continue
            val = t.val
            if t.dma:
                ds = self.dsems[id(t.sem)]
                val = ds.total
                ds.open = False
            if self.seen[eng].get(id(t.sem), 0) >= val:
                continue
            self.seen[eng][id(t.sem)] = val
            out.append((t.sem, val))
        return out

    def _mark(self, tok, reads, writes):
        for b in reads:
            b.r[id(tok.sem)] = tok
        for b in writes:
            if b.multi:
                b.w[id(tok.sem)] = tok
            else:
                b.w = {id(tok.sem): tok}
                b.r = {}

    def _op(self, eng, fn, reads=(), writes=()):
        waits = self._collect(eng, reads, writes)
        if self.cnt[eng] >= SEM_ROLL:
            self.sem[eng] = self._new_sem("c_" + eng)
            self.cnt[eng] = 0
        self.cnt[eng] += 1
        tok = Tok(self.sem[eng], self.cnt[eng])

        def run(h, waits=waits, fn=fn, tok=tok):
            for s, v in waits:
                h.wait_ge(s, v)
            fn(h).then_inc(tok.sem, 1)

        self.ops[eng].append(run)
        self._mark(tok, reads, writes)
        self.nops += 1
        return tok

    def _dma(self, eng, out_buf, out_ap, in_buf, in_ap, owner=None, **kw):
        if owner is None:
            owner = in_buf if (out_buf.is_dram and not in_buf.is_dram) else out_buf
        if owner.dsem is None:
            owner.dsem = DSem(self._new_sem("d_" + owner.name))
            self.dsems[id(owner.dsem.sem)] = owner.dsem
        ds = owner.dsem
        waits = self._collect(eng, [in_buf], [out_buf])
        if not ds.open and ds.total > 0:
            if self.seen[eng].get(id(ds.sem), 0) < ds.total:
                self.seen[eng][id(ds.sem)] = ds.total
                waits.append((ds.sem, ds.total))
        ds.total += 16
        ds.open = True
        tok = Tok(ds.sem, ds.total, dma=True)

        def run(h, waits=waits, tok=tok, out_ap=out_ap, in_ap=in_ap, kw=kw):
            for s, v in waits:
                h.wait_ge(s, v)
            h.dma_start(out=out_ap, in_=in_ap, **kw).then_inc(tok.sem, 16)

        self.ops[eng].append(run)
        self._mark(tok, [in_buf], [out_buf])
        self.nops += 1
        return tok

    def _finish(self, eng, bufs):
        waits = self._collect(eng, bufs, [])

        def run(h, waits=waits):
            for s, v in waits:
                h.wait_ge(s, v)

        self.ops[eng].append(run)

    def _emit(self):
        nc = self.nc
        ops = self.ops
        with nc.Block() as block:

            @block.tensor
            def _(h):
                for f in ops["tensor"]:
                    f(h)

            @block.vector
            def _(h):
                for f in ops["vector"]:
                    f(h)

            @block.scalar
            def _(h):
                for f in ops["scalar"]:
                    f(h)

            @block.gpsimd
            def _(h):
                for f in ops["gpsimd"]:
                    f(h)

            @block.sync
            def _(h):
                for f in ops["sync"]:
                    f(h)

        self.stack.close()

    def op(self, eng, fn, reads=(), writes=(), est=300.0):
        self.rec.append(("op", eng, fn, list(reads), list(writes), float(est)))

    def dma(self, eng, out_buf, out_ap, in_buf, in_ap, owner=None, est=None, **kw):
        if est is None:
            try:
                nb = 1
                for d_ in out_ap.shape:
                    nb *= int(d_)
                nb *= 2 if out_ap.dtype == BF16 else 4
            except Exception:
                nb = 65536
            est = 2000.0 + nb / 150.0
        self.rec.append(("dma", eng, (out_buf, out_ap, in_buf, in_ap, owner, kw), [in_buf], [out_buf], float(est)))

    def finish(self, eng, bufs):
        self.rec.append(("finish", eng, list(bufs)))

    def barrier(self):
        self.rec.append(("barrier",))

    def set_filler(self, spec):
        self.rec.append(("filler", spec))

    def _barrier(self):
        toks = []
        for e in ("tensor", "vector", "scalar", "gpsimd"):
            if self.cnt[e] > 0:
                toks.append((self.sem[e], self.cnt[e], e))
        dtoks = []
        for ds in self.dsems.values():
            if ds.total > 0:
                dtoks.append((ds.sem, ds.total))
                ds.open = False
        for eng in ENGS:
            waits = []
            for s_, v, e in toks:
                if e == eng:
                    continue
                if self.seen[eng].get(id(s_), 0) >= v:
                    continue
                self.seen[eng][id(s_)] = v
                waits.append((s_, v))
            for s_, v in dtoks:
                if self.seen[eng].get(id(s_), 0) >= v:
                    continue
                self.seen[eng][id(s_)] = v
                waits.append((s_, v))

            def run(h, waits=waits):
                for s2, v2 in waits:
                    h.wait_ge(s2, v2)

            self.ops[eng].append(run)

    def _schedule(self, seg):
        import heapq
        n = len(seg)
        preds = [set() for _ in range(n)]
        lastw, readers = {}, {}
        for i, it in enumerate(seg):
            reads, writes = it[3], it[4]
            for b in reads:
                preds[i].update(lastw.get(id(b), ()))
            for b in writes:
                if b.multi:
                    continue
                preds[i].update(lastw.get(id(b), ()))
                preds[i].update(readers.get(id(b), ()))
            for b in reads:
                readers.setdefault(id(b), []).append(i)
            for b in writes:
                if b.multi:
                    lastw.setdefault(id(b), []).append(i)
                else:
                    lastw[id(b)] = [i]
                    readers[id(b)] = []
            preds[i].discard(i)
        succ = [[] for _ in range(n)]
        indeg = [0] * n
        for i in range(n):
            indeg[i] = len(preds[i])
            for p in preds[i]:
                succ[p].append(i)
        import os
        import os
        LAT = float(os.environ.get('SCHED_LAT', '150'))
        PRIO = os.environ.get('SCHED_PRIO', '1') == '1'
        blev = [0.0] * n
        for i in range(n - 1, -1, -1):
            it = seg[i]
            d_ = it[5] + (2000.0 if it[0] == "dma" else 0.0)
            m_ = 0.0
            for s_ in succ[i]:
                if blev[s_] > m_:
                    m_ = blev[s_]
            blev[i] = d_ + m_
        fin = [0.0] * n
        efree = {e: 0.0 for e in ENGS}
        waitq = {e: [] for e in ENGS}
        readyq = {e: [] for e in ENGS}

        def push(i):
            e = seg[i][1]
            rt = 0.0
            for p in preds[i]:
                t = fin[p] + (LAT if seg[p][1] != e else 0.0)
                if t > rt:
                    rt = t
            heapq.heappush(waitq[e], (rt, i))

        def cand(e):
            w, r = waitq[e], readyq[e]
            while w and w[0][0] <= efree[e]:
                rt, i = heapq.heappop(w)
                heapq.heappush(r, ((-blev[i] if PRIO else i), i))
            if r:
                return (efree[e], r[0][0], r[0][1], True)
            if w:
                return (w[0][0], (-blev[w[0][1]] if PRIO else w[0][1]), w[0][1], False)
            return None

        for i in range(n):
            if indeg[i] == 0:
                push(i)
        order = []
        nreal = 0
        fill = self.cur_filler
        while nreal < n:
            best = None
            for e in ENGS:
                c_ = cand(e)
                if c_ is not None and (best is None or (c_[0], c_[1]) < (best[0][0], best[0][1])):
                    best = (c_, e)
            (st, _, i, from_ready), e = best
            if fill is not None and efree[fill[0]] > 0.0:
                fe, gap = fill[0], fill[5]
                while st - efree[fe] > gap:
                    order.append(-1)
                    efree[fe] += fill[4]
                    self.nfill += 1
            if from_ready:
                heapq.heappop(readyq[e])
            else:
                heapq.heappop(waitq[e])
            nreal += 1
            it = seg[i]
            if it[0] == "dma":
                issue = 500.0 if e == "sync" else 800.0
                efree[e] = st + issue
                fin[i] = st + issue + it[5]
            else:
                efree[e] = st + it[5]
                fin[i] = efree[e]
            order.append(i)
            for s_ in succ[i]:
                indeg[s_] -= 1
                if indeg[s_] == 0:
                    push(s_)
        return order

    def _flush(self, seg):
        if not seg:
            return
        order = self._schedule(seg) if self.sched else range(len(seg))
        for i in order:
            if i < 0:
                f_ = self.cur_filler
                self._op(f_[0], f_[1], f_[2], f_[3])
                continue
            it = seg[i]
            if it[0] == "op":
                self._op(it[1], it[2], it[3], it[4])
            else:
                out_buf, out_ap, in_buf, in_ap, owner, kw = it[2]
                self._dma(it[1], out_buf, out_ap, in_buf, in_ap, owner=owner, **kw)

    def emit(self):
        seg = []
        self.cur_filler = None
        for it in self.rec:
            if it[0] == "filler":
                self._flush(seg)
                seg = []
                self.cur_filler = it[1]
            elif it[0] == "barrier":
                self._flush(seg)
                seg = []
                self._barrier()
            elif it[0] == "finish":
                self._flush(seg)
                seg = []
                self._finish(it[1], it[2])
            else:
                seg.append(it)
        self._flush(seg)
        self._emit()

import math
import numpy as np

S = 4096
D = 1024
DFF = 4096
NBLK = S // 128
EPS = 1e-6
C1 = 6.28125
C2 = 2.0 * math.pi - 6.28125


class Arena:
    def __init__(self, P, nbytes):
        self.P = P
        self.base = P.sbuf("arena", [128, nbytes // 4], F32)
        self.top = 0
        self.cap = nbytes
        self.peak = 0

    def alloc(self, name, shape, dt):
        esz = 2 if dt == BF16 else 4
        nel = int(np.prod(shape[1:]))
        n = (nel * esz + 31) // 32 * 32
        off = self.top
        self.top += n
        assert self.top <= self.cap, ("arena overflow", name, self.top, self.cap)
        self.peak = max(self.peak, self.top)
        ap = self.base.t[0:shape[0], off // 4:(off + n) // 4]
        if dt != F32:
            ap = ap.bitcast(dt)
        ap = ap[:, 0:nel]
        if len(shape) == 3:
            ap = ap.rearrange("p (a b) -> p a b", a=shape[1])
        elif len(shape) == 4:
            ap = ap.rearrange("p (a b c) -> p a b c", a=shape[1], b=shape[2])
        return Buf(ap, name)

    def mark(self):
        return self.top

    def release(self, m):
        print("arena phase peak", self.peak, "release to", m)
        self.peak = m
        self.top = m


class Ctx:
    pass


def barrier(P):
    P.barrier()


def _fd(ap):
    n = 1
    for d_ in ap.shape[1:]:
        n *= int(d_)
    return n


def mm(C, ob, oap, lb, lap, rb, rap, start=True, stop=True):
    n = _fd(rap)
    import os
    est = (4.0 * n / 2.4 + 150.0) if lap.dtype == F32 else (n * float(os.environ.get('MM_NS', '0.45')) + 15.0)
    C.P.op("tensor", lambda h: h.matmul(oap, lhsT=lap, rhs=rap, start=start, stop=stop), [lb, rb], [ob], est=est)


def tr(C, ob, oap, ib, iap, idb, idap):
    C.P.op("tensor", lambda h: h.transpose(oap, iap, idap), [ib, idb], [ob], est=110.0)


def act(C, ob, oap, ib, iap, func, scale=1.0, bias=None, accum=None, rd=(), wr=()):
    kw = {}
    if bias is not None:
        kw["bias"] = bias
    if accum is not None:
        kw["accum_out"] = accum
    est = 230.0 + 0.83 * _fd(iap) + (90.0 if accum is not None else 0.0)
    C.P.op("scalar", lambda h: h.activation(out=oap, in_=iap, func=func, scale=scale, **kw),
           [ib] + list(rd), [ob] + list(wr), est=est)


def vec(C, name, kw, rd, wr, eng="vector"):
    o = kw.get("out", kw.get("ap"))
    n = _fd(o)
    if eng == "gpsimd":
        est = 250.0 + 2.3 * n
    elif name == "reciprocal":
        est = 120.0 + 6.6 * n
    elif name == "tensor_tensor_scan":
        est = 150.0 + 2.1 * n
    else:
        est = 110.0 + 1.0 * n
    C.P.op(eng, lambda h, name=name, kw=kw: getattr(h, name)(**kw), list(rd), list(wr), est=est)


def load_w(C, name, wd, kch, col0, ncols, eng="gpsimd", wb=None):
    if wb is None:
        wb = C.A.alloc(name, [128, kch, ncols], BF16)
    wb.multi = True
    if wd.name in C.WB:
        wd = C.WB[wd.name]
        eng = "sync"
    src = wd.t.rearrange("(c p) n -> p c n", p=128)
    step = 1 if ncols * 2 >= 4096 else 4
    for c in range(0, kch, step):
        c1 = min(kch, c + step)
        C.P.dma(eng, wb, wb[:, c:c1, :], wd, src[:, c:c1, col0:col0 + ncols])
    return wb


def bg_cast(C, names):
    owner = None
    for nm in names:
        wd = C.I[nm]
        rows, cols = wd.t.shape
        wb = C.P.dram(nm + "_bf", [rows, cols], BF16)
        wb.multi = True
        if owner is None:
            owner = wb
        nchunk = max(1, rows * cols // (1024 * 1024))
        rstep = rows // nchunk
        for r in range(0, rows, rstep):
            C.P.dma("gpsimd", wb, wb[r:r + rstep, :], wd, wd[r:r + rstep, :], owner=owner)
        C.WB[nm] = wb


def bank(C):
    b = C.pb[C.bi % getattr(C, 'nbank', 8)]
    C.bi += 1
    return b


def rms_rstd(C, ss, tmp, rstd, n):
    (sb, sap), (tb, tap), (rb, rap) = ss, tmp, rstd
    np_ = sap.shape[0]
    act(C, tb, tap, sb, sap, AF.Ln, scale=1.0 / n, bias=C.eps[0:np_, 0:1], rd=[C.eps])
    act(C, rb, rap, tb, tap, AF.Exp, scale=-0.5)


def norm_transpose(C, xin, xin_ap, gcol0, hT, col0, tmps):
    junk, ss, t1, rstd, xn = tmps
    act(C, junk, junk[:], xin, xin_ap, AF.Square, accum=ss[:, 0:1], wr=[ss])
    rms_rstd(C, (ss, ss[:, 0:1]), (t1, t1[:, 0:1]), (rstd, rstd[:, 0:1]), D)
    vec(C, "tensor_scalar", dict(out=xn[:], in0=xin_ap, scalar1=rstd[:, 0:1], scalar2=None, op0=ALU.mult),
        [xin, rstd], [xn])
    pb = bank(C)
    pbv = pb.t[:, 0:512].bitcast(BF16)
    for c in range(8):
        tr(C, pb, pbv[:, c * 128:(c + 1) * 128], xn, xn[:, c * 128:(c + 1) * 128], C.ident_b, C.ident_b[:])
    gap = C.gains[:, gcol0:gcol0 + 8].unsqueeze(2).to_broadcast([128, 8, 128])
    vec(C, "tensor_tensor", dict(out=hT[:, 0:8, col0:col0 + 128],
                                     in0=pbv.rearrange("p (c t) -> p c t", c=8), in1=gap, op=ALU.mult),
        [pb, C.gains], [hT])


def pe_filler(C, bankbuf, on=True, gap=40.0, est=40.0):
    import os
    if not on or os.environ.get("NOFILL"):
        C.P.set_filler(None)
        return
    dst = bankbuf[:, 0:64]
    dummy = Buf(dst, "pe_dummy")
    dummy.multi = True
    fn = lambda h: h.matmul(dst, lhsT=C.ident_b[:, 0:128], rhs=C.ident_b[:, 0:64], start=True, stop=True)
    C.P.set_filler(("tensor", fn, [C.ident_b], [dummy], float(os.environ.get("FILL_EST", est)), float(os.environ.get("FILL_GAP", gap))))


def phase0(C):
    P, A, I = C.P, C.A, C.I
    C.ident_f = A.alloc("ident_f", [128, 128], F32)
    P.dma("sync", C.ident_f, C.ident_f[:], I["ident"], I["ident"][:])
    C.ident_b = A.alloc("ident_b", [128, 128], BF16)
    vec(C, "tensor_copy", dict(out=C.ident_b[:], in_=C.ident_f[:]), [C.ident_f], [C.ident_b])
    C.tri_f = A.alloc("tri_f", [128, 128], F32)
    P.dma("sync", C.tri_f, C.tri_f[:], I["tri"], I["tri"][:])
    C.tri_b = A.alloc("tri_b", [128, 128], BF16)
    vec(C, "tensor_copy", dict(out=C.tri_b[:], in_=C.tri_f[:]), [C.tri_f], [C.tri_b])
    C.i4 = A.alloc("i4", [4, 4], F32)
    P.dma("sync", C.i4, C.i4[:], I["ident"], I["ident"][0:4, 0:4])
    C.ones4 = A.alloc("ones4", [4, 512], F32)
    vec(C, "memset", dict(ap=C.ones4[:], constant=1.0), [], [C.ones4])
    C.eps = A.alloc("eps", [128, 1], F32)
    vec(C, "memset", dict(ap=C.eps[:], constant=EPS), [], [C.eps])
    C.one = A.alloc("one", [128, 1], F32)
    vec(C, "memset", dict(ap=C.one[:], constant=1.0), [], [C.one])
    C.gains = A.alloc("gains", [128, 40], F32)
    P.dma("sync", C.gains, C.gains[:], I["gains"], I["gains"][:])
    C.smalls = A.alloc("smalls", [128, 8], F32)
    P.dma("sync", C.smalls, C.smalls[:], I["smalls"], I["smalls"][:])
    C.mem_kT = A.alloc("mem_kT", [128, 2, 256], BF16)
    C.vpad = A.alloc("vpad", [128, 2, 4, 128], BF16)
    C.onespad = A.alloc("onespad", [128, 2, 128], BF16)
    vec(C, "memset", dict(ap=C.vpad[:], constant=0.0), [], [C.vpad])
    vec(C, "memset", dict(ap=C.onespad[:], constant=0.0), [], [C.onespad])
    vec(C, "memset", dict(ap=C.onespad[:, 0, 0:64], constant=1.0), [], [C.onespad])
    vec(C, "memset", dict(ap=C.onespad[:, 1, 64:128], constant=1.0), [], [C.onespad])

    m0 = A.mark()
    wkv = load_w(C, "wkv", I["w_mem_kv"], 8, 0, 512)
    memT = A.alloc("memT", [128, 8, 256], BF16)
    junk = A.alloc("junk", [128, 1024], BF16)
    ss = A.alloc("ss", [128, 1], F32)
    t1 = A.alloc("t1", [128, 1], F32)
    rstd = A.alloc("rstd", [128, 1], F32)
    xn = A.alloc("xn", [128, 1024], BF16)
    for mb in range(2):
        mx = A.alloc("memx%d" % mb, [128, 1024], F32)
        P.dma("sync", mx, mx[:], I["mem"], I["mem"][mb * 128:(mb + 1) * 128, :])
        norm_transpose(C, mx, mx[:], 32, memT, mb * 128, (junk, ss, t1, rstd, xn))
    for pair in range(2):
        pb = bank(C)
        for c in range(8):
            mm(C, pb, pb[:, 0:256], wkv, wkv[:, c, pair * 128:(pair + 1) * 128], memT, memT[:, c, :],
               start=(c == 0), stop=(c == 7))
        act(C, C.mem_kT, C.mem_kT[:, pair, :], pb, pb[:, 0:256], AF.Copy)
    for mb in range(2):
        pb = bank(C)
        for c in range(8):
            mm(C, pb, pb[:, 0:256], memT, memT[:, c, mb * 128:(mb + 1) * 128], wkv, wkv[:, c, 256:512],
               start=(c == 0), stop=(c == 7))
        for hh in range(4):
            i = hh % 2
            act(C, C.vpad, C.vpad[:, mb, hh, 64 * i:64 * i + 64], pb, pb[:, hh * 64:(hh + 1) * 64], AF.Copy)

    CH = 1024
    posi = A.alloc("posi", [128, CH], I32)
    posf = A.alloc("posf", [128, CH], F32)
    ang = A.alloc("ang", [128, CH], F32)
    a2 = A.alloc("a2", [128, CH], F32)
    kq = A.alloc("kq", [128, CH], I32)
    kf = A.alloc("kf", [128, CH], F32)
    rr = A.alloc("rr", [128, CH], F32)
    so = [A.alloc("so%d" % i, [128, CH], F32) for i in range(2)]
    PI = math.pi
    k = 0
    for ch in range(S // CH):
        cs = slice(ch * CH, (ch + 1) * CH)
        P.dma("sync", posi, posi[:], I["pos"], I["pos"][0:1, cs].to_broadcast([128, CH]))
        vec(C, "tensor_copy", dict(out=posf[:], in_=posi[:]), [posi], [posf])
        vec(C, "tensor_scalar", dict(out=ang[:], in0=posf[:], scalar1=C.smalls[:, 5:6], scalar2=None,
                                         op0=ALU.mult), [posf, C.smalls], [ang])
        for which in range(2):
            src = ang
            if which == 1:
                vec(C, "tensor_scalar", dict(out=a2[:], in0=ang[:], scalar1=PI / 2, scalar2=None, op0=ALU.add),
                    [ang], [a2])
                src = a2
            vec(C, "tensor_scalar", dict(out=kq[:], in0=src[:], scalar1=1.0 / (2 * PI), scalar2=None,
                                                      op0=ALU.mult), [src], [kq])
            vec(C, "tensor_copy", dict(out=kf[:], in_=kq[:]), [kq], [kf])
            vec(C, "scalar_tensor_tensor", dict(out=rr[:], in0=kf[:], scalar=-C1, in1=src[:],
                                                             op0=ALU.mult, op1=ALU.add), [kf, src], [rr])
            vec(C, "scalar_tensor_tensor", dict(out=rr[:], in0=kf[:], scalar=-C2, in1=rr[:],
                                                    op0=ALU.mult, op1=ALU.add), [kf, rr], [rr])
            vec(C, "tensor_scalar", dict(out=rr[:], in0=rr[:], scalar1=-PI, scalar2=PI, op0=ALU.max,
                                             op1=ALU.min), [rr], [rr])
            o = so[k % 2]
            k += 1
            act(C, o, o[:], rr, rr[:], AF.Sin)
            if which == 0:
                vec(C, "tensor_scalar", dict(out=o[:], in0=o[:], scalar1=C.smalls[:, 6:7], scalar2=None,
                                                      op0=ALU.mult), [o, C.smalls], [o])
                P.dma("sync", C.ROPES, C.ROPES[:, cs], o, o[:])
            else:
                P.dma("sync", C.ROPEC, C.ROPEC[:, cs], o, o[:])
    barrier(P)
    A.release(m0)


QK0, V0, O0, IG0, FG0, QM0 = 0, 768, 1536, 2304, 2308, 2312


def l0mix(C, T=256):
    P, A, I = C.P, C.A, C.I
    NB = T // 128
    NT = S // T
    m0 = A.mark()
    C.nbank = 5
    pe_filler(C, C.pb[5])
    w_in = load_w(C, "w_in0", I["w_in0"], 8, 0, 2568)
    w_out = load_w(C, "w_out0", I["w_out0"], 8, 0, 1024)
    if C.bgcast:
        bg_cast(C, ["w_ff1_0", "w_ff2_0"])
        bg_cast(C, ["w_in1", "w_krB", "w_uq_nope", "w_uq_rA", "w_uq_rB", "w_uk", "w_uv"])
        bg_cast(C, ["w_ff1_1", "w_ff2_1", "w_out1"])
    convT = A.alloc("convT", [96, 32], F32)
    P.dma("sync", convT, convT[:], I["convT"], I["convT"][:])
    bif = A.alloc("bif", [4, 2], F32)
    P.dma("sync", bif, bif[:], I["bif"], I["bif"][:])
    whn = A.alloc("whn", [128, 768], F32)
    P.dma("sync", whn, whn[:], I["whn"], I["whn"][0:1, :].to_broadcast([128, 768]))
    Cn = A.alloc("Cn", [96, 4, 193], F32)
    vec(C, "memset", dict(ap=Cn[:], constant=0.0), [], [Cn])
    halo = A.alloc("halo", [96, 8, 3], F32)
    vec(C, "memset", dict(ap=halo[:], constant=0.0), [], [halo])
    Blast = A.alloc("Blast", [4, 1], F32)
    Glast = A.alloc("Glast", [4, 1], F32)
    vec(C, "memset", dict(ap=Blast[:], constant=0.0), [], [Blast])
    vec(C, "memset", dict(ap=Glast[:], constant=0.0), [], [Glast])

    S1 = []
    for i in range(2):
        d = Ctx()
        d.hT = A.alloc("hT%d" % i, [128, 8, T], BF16)
        d.qkT = A.alloc("qkT%d" % i, [96, 8, T], BF16)
        d.vaug = A.alloc("vaug%d" % i, [128, NB, 4, 193], BF16)
        vec(C, "memset", dict(ap=d.vaug[:], constant=1.0), [], [d.vaug])
        d.TM = A.alloc("TM%d" % i, [128, NB * 8], F32)
        d.DEC = A.alloc("DEC%d" % i, [128, NB * 4], F32)
        S1.append(d)
    qmT3 = [A.alloc("qmT%d" % i, [128, 2, T], BF16) for i in range(3)]
    xin = [A.alloc("xin%d" % i, [128, 1024], F32) for i in range(2)]
    junk = A.alloc("junk", [128, 1024], BF16)
    xn = [A.alloc("xn%d" % i, [128, 1024], BF16) for i in range(2)]
    ss = [A.alloc("ss%d" % i, [128, 1], F32) for i in range(2)]
    t1 = [A.alloc("t1%d" % i, [128, 1], F32) for i in range(2)]
    rstd = [A.alloc("rstd%d" % i, [128, 1], F32) for i in range(2)]
    pre = [A.alloc("pre%d" % i, [96, T + 3], F32) for i in range(2)]
    cacc = [A.alloc("cacc%d" % i, [96, T], F32) for i in range(2)]
    ce = [A.alloc("ce%d" % i, [96, T], F32) for i in range(2)]
    rows = {}
    for nm in ("ig", "z", "a", "e", "mn", "lf", "Bc", "gp", "Gc", "ur", "pr"):
        rows[nm] = A.alloc("row_" + nm, [4, T], F32)
    murho = A.alloc("murho", [4, 2, NB], F32)
    dd = A.alloc("dd", [4, NB], F32)
    Rdec = A.alloc("Rdec", [4, NB, 4], F32)
    og = [A.alloc("og%d" % i, [128, 768], F32) for i in range(2)]
    wgo = [A.alloc("wgo%d" % i, [128, 768], BF16) for i in range(2)]
    sT = [A.alloc("sT%d" % i, [128, 4, 128], BF16) for i in range(2)]
    uk = [A.alloc("uk%d" % i, [128, 4, 96], BF16) for i in range(2)]
    Dbf = [A.alloc("Dbf%d" % i, [96, 2, 193], BF16) for i in range(4)]
    yml = [A.alloc("yml%d" % i, [128, 768], BF16) for i in range(2)]
    yT = [A.alloc("yT%d" % i, [128, 8, T], BF16) for i in range(2)]
    eT = [A.alloc("eT%d" % i, [128, T], BF16) for i in range(4)]
    rec = [A.alloc("rec%d" % i, [128, T], F32) for i in range(2)]
    xres = [A.alloc("xres%d" % i, [128, 1024], F32) for i in range(2)]
    xo = [A.alloc("xo%d" % i, [128, 1024], F32) for i in range(2)]
    junk2 = A.alloc("junk2", [128, 192], BF16)
    den = [A.alloc("den%d" % i, [128, 2], F32) for i in range(4)]
    rcp = [A.alloc("rcp%d" % i, [128, 2], F32) for i in range(4)]
    ss2 = [A.alloc("ss2%d" % i, [128, 2], F32) for i in range(4)]
    t2 = [A.alloc("t2%d" % i, [128, 2], F32) for i in range(4)]
    rs2 = [A.alloc("rs2%d" % i, [128, 2], F32) for i in range(4)]
    cnt = {"x": 0, "g": 0, "e": 0, "p": 0}

    def qk_group(s, g):
        k = g % 2
        pb = bank(C)
        for c in range(8):
            mm(C, pb, pb[0:96, 0:T], w_in, w_in[:, c, QK0 + g * 96:QK0 + (g + 1) * 96], s.hT, s.hT[:, c, :],
               start=(c == 0), stop=(c == 7))
        pr, ca, e = pre[k], cacc[k], ce[k]
        vec(C, "tensor_copy", dict(out=pr[:, 0:3], in_=halo[:, g, :]), [halo], [pr])
        act(C, pr, pr[:, 3:3 + T], pb, pb[0:96, 0:T], AF.Copy)
        yield
        vec(C, "tensor_copy", dict(out=halo[:, g, :], in_=pr[:, T:T + 3]), [pr], [halo])
        vec(C, "tensor_scalar", dict(out=ca[:], in0=pr[:, 3:3 + T], scalar1=convT[:, g * 4 + 3:g * 4 + 4], scalar2=None,
                                     op0=ALU.mult), [pr, convT], [ca])
        yield
        for j in (2, 1, 0):
            vec(C, "scalar_tensor_tensor", dict(
                out=ca[:], in0=pr[:, j:j + T], scalar=convT[:, g * 4 + j:g * 4 + j + 1], in1=ca[:],
                op0=ALU.mult, op1=ALU.add), [pr, convT, ca], [ca])
            yield
        act(C, e, e[:], ca, ca[:], AF.Exp, scale=-1.0)
        yield
        act(C, e, e[:], e, e[:], AF.Ln, bias=C.one[0:96, 0:1], rd=[C.one])
        yield
        act(C, e, e[:], e, e[:], AF.Exp, scale=-1.0)
        yield
        cst = 1.0 if g < 4 else 96.0 ** -0.5
        vec(C, "scalar_tensor_tensor", dict(
            out=s.qkT[:, g, :], in0=ca[:], scalar=cst, in1=e[:], op0=ALU.mult, op1=ALU.mult), [ca, e], [s.qkT])

    def stage1(t):
        s = S1[t % 2]
        for b in range(NB):
            k = cnt["x"] % 2
            cnt["x"] += 1
            r0 = t * T + b * 128
            P.dma("sync", xin[k], xin[k][:], I["x"], I["x"][r0:r0 + 128, :])
            norm_transpose(C, xin[k], xin[k][:], 0, s.hT, b * 128, (junk, ss[k], t1[k], rstd[k], xn[k]))
            yield
        for g in range(0, 8, 2):
            ga, gb = qk_group(s, g), qk_group(s, g + 1)
            for _ in ga:
                next(gb, None)
            for _ in gb:
                pass
            yield
        R = rows
        pbg = bank(C)
        for c in range(8):
            mm(C, pbg, pbg[0:4, 0:T], w_in, w_in[:, c, IG0:IG0 + 4], s.hT, s.hT[:, c, :], start=(c == 0), stop=(c == 7))
        act(C, R["ig"], R["ig"][:], pbg, pbg[0:4, 0:T], AF.Identity, bias=bif[:, 0:1], rd=[bif])
        pbf = bank(C)
        for c in range(8):
            mm(C, pbf, pbf[0:4, 0:T], w_in, w_in[:, c, FG0:FG0 + 4], s.hT, s.hT[:, c, :], start=(c == 0), stop=(c == 7))
        act(C, R["z"], R["z"][:], pbf, pbf[0:4, 0:T], AF.Identity, bias=bif[:, 1:2], rd=[bif])
        act(C, R["a"], R["a"][:], R["z"], R["z"][:], AF.Abs)
        act(C, R["e"], R["e"][:], R["a"], R["a"][:], AF.Exp, scale=-1.0)
        act(C, R["e"], R["e"][:], R["e"], R["e"][:], AF.Ln, bias=C.one[0:4, 0:1], rd=[C.one])
        vec(C, "tensor_scalar", dict(out=R["mn"][:], in0=R["z"][:], scalar1=0.0, scalar2=None, op0=ALU.min),
            [R["z"]], [R["mn"]])
        vec(C, "tensor_tensor", dict(out=R["lf"][:], in0=R["mn"][:], in1=R["e"][:], op=ALU.subtract),
            [R["mn"], R["e"]], [R["lf"]])
        vec(C, "tensor_tensor_scan", dict(out=R["Bc"][:], data0=C.ones4[:, 0:T], data1=R["lf"][:],
                                          initial=Blast[:, 0:1], op0=ALU.mult, op1=ALU.add),
            [C.ones4, R["lf"], Blast], [R["Bc"]])
        vec(C, "tensor_tensor", dict(out=R["gp"][:], in0=R["ig"][:], in1=R["Bc"][:], op=ALU.subtract),
            [R["ig"], R["Bc"]], [R["gp"]])
        vec(C, "tensor_tensor_scan", dict(out=R["Gc"][:], data0=R["gp"][:], data1=R["gp"][:],
                                          initial=Glast[:, 0:1], op0=ALU.max, op1=ALU.max),
            [R["gp"], Glast], [R["Gc"]])
        yield
        Gv = R["Gc"][:].rearrange("p (b t) -> p b t", b=NB)
        vec(C, "tensor_copy", dict(out=murho[:, 1, :], in_=Gv[:, :, 127]), [R["Gc"]], [murho])
        vec(C, "tensor_copy", dict(out=murho[:, 0, 0:1], in_=Glast[:, 0:1]), [Glast], [murho])
        if NB > 1:
            vec(C, "tensor_copy", dict(out=murho[:, 0, 1:NB], in_=murho[:, 1, 0:NB - 1]), [murho], [murho])
        vec(C, "tensor_copy", dict(out=Blast[:, 0:1], in_=R["Bc"][:, T - 1:T]), [R["Bc"]], [Blast])
        vec(C, "tensor_copy", dict(out=Glast[:, 0:1], in_=R["Gc"][:, T - 1:T]), [R["Gc"]], [Glast])
        rho_bc = murho[:, 1, :].unsqueeze(2).to_broadcast([4, NB, 128])
        vec(C, "tensor_tensor", dict(out=R["ur"][:].rearrange("p (b t) -> p b t", b=NB),
                                     in0=R["gp"][:].rearrange("p (b t) -> p b t", b=NB), in1=rho_bc,
                                     op=ALU.subtract), [R["gp"], murho], [R["ur"]])
        act(C, R["ur"], R["ur"][:], R["ur"], R["ur"][:], AF.Exp)
        vec(C, "tensor_tensor", dict(out=R["pr"][:].rearrange("p (b t) -> p b t", b=NB),
                                     in0=R["Bc"][:].rearrange("p (b t) -> p b t", b=NB), in1=rho_bc,
                                     op=ALU.add), [R["Bc"], murho], [R["pr"]])
        act(C, R["pr"], R["pr"][:], R["pr"], R["pr"][:], AF.Exp, scale=-1.0)
        vec(C, "tensor_tensor", dict(out=dd[:], in0=murho[:, 0, :], in1=murho[:, 1, :], op=ALU.subtract),
            [murho], [dd])
        act(C, dd, dd[:], dd, dd[:], AF.Exp)
        yield
        pbt = bank(C)
        for b in range(NB):
            mm(C, pbt, pbt[:, b * 8:b * 8 + 4], R["ur"], R["ur"][:, b * 128:(b + 1) * 128], C.i4, C.i4[:])
            mm(C, pbt, pbt[:, b * 8 + 4:b * 8 + 8], R["pr"], R["pr"][:, b * 128:(b + 1) * 128], C.i4, C.i4[:])
        vec(C, "tensor_copy", dict(out=s.TM[:], in_=pbt[:, 0:NB * 8]), [pbt], [s.TM])
        vec(C, "tensor_tensor", dict(out=Rdec[:], in0=dd[:].unsqueeze(2).to_broadcast([4, NB, 4]),
                                     in1=C.i4[:].unsqueeze(1).to_broadcast([4, NB, 4]), op=ALU.mult),
            [dd, C.i4], [Rdec])
        pbd = bank(C)
        mm(C, pbd, pbd[:, 0:NB * 4], C.ones4, C.ones4[:, 0:128], Rdec, Rdec[:].rearrange("p b h -> p (b h)"))
        vec(C, "tensor_copy", dict(out=s.DEC[:], in_=pbd[:, 0:NB * 4]), [pbd], [s.DEC])
        yield
        for b in range(NB):
            for half in range(2):
                pb = bank(C)
                for c in range(8):
                    mm(C, pb, pb[:, 0:384], s.hT, s.hT[:, c, b * 128:(b + 1) * 128], w_in,
                       w_in[:, c, V0 + half * 384:V0 + (half + 1) * 384], start=(c == 0), stop=(c == 7))
                act(C, s.vaug, s.vaug[:, b, 2 * half:2 * half + 2, 0:192], pb,
                    pb[:, 0:384].rearrange("p (a d) -> p a d", a=2), AF.Copy)
            yield
        for pair in range(2):
            pb = bank(C)
            for c in range(8):
                mm(C, pb, pb[:, 0:T], w_in, w_in[:, c, QM0 + pair * 128:QM0 + (pair + 1) * 128], s.hT, s.hT[:, c, :],
                   start=(c == 0), stop=(c == 7))
            act(C, qmT3[t % 3], qmT3[t % 3][:, pair, :], pb, pb[:, 0:T], AF.Copy)

    def stage2(t):
        s = S1[t % 2]
        y = yT[t % 2]

        def F(b):
            blk = t * NB + b
            k = blk % 2
            cs = slice(b * 128, (b + 1) * 128)
            for half in range(2):
                pb = bank(C)
                for c in range(8):
                    mm(C, pb, pb[:, 0:384], s.hT, s.hT[:, c, cs], w_in,
                       w_in[:, c, O0 + half * 384:O0 + (half + 1) * 384], start=(c == 0), stop=(c == 7))
                act(C, og[k], og[k][:, half * 384:(half + 1) * 384], pb, pb[:, 0:384], AF.Exp, scale=-1.0)
            pst = bank(C)
            for hh in range(4):
                mm(C, pst, pst[:, hh * 128:(hh + 1) * 128], s.qkT, s.qkT[:, 4 + hh, cs], s.qkT, s.qkT[:, hh, cs])
            ptk = bank(C)
            ptkv = ptk.t[:, 0:512].bitcast(BF16)
            for hh in range(4):
                tr(C, ptk, ptkv[:, hh * 96:(hh + 1) * 96], s.qkT, s.qkT[:, 4 + hh, cs], C.ident_b, C.ident_b[0:96, 0:96])
            for hh in range(4):
                vec(C, "scalar_tensor_tensor", dict(
                    out=sT[k][:, hh, :], in0=pst[:, hh * 128:(hh + 1) * 128], scalar=s.TM[:, b * 8 + hh:b * 8 + hh + 1],
                    in1=C.tri_f[:], op0=ALU.mult, op1=ALU.mult), [pst, s.TM, C.tri_f], [sT[k]])
                act(C, uk[k], uk[k][:, hh, :], ptk, ptkv[:, hh * 96:(hh + 1) * 96], AF.Copy,
                    scale=s.TM[:, b * 8 + hh:b * 8 + hh + 1], rd=[s.TM])
            act(C, og[k], og[k][:], og[k], og[k][:], AF.Ln, bias=C.one[:, 0:1], rd=[C.one])
            act(C, og[k], og[k][:], og[k], og[k][:], AF.Exp, scale=-1.0)
            vec(C, "tensor_tensor", dict(out=wgo[k][:], in0=og[k][:], in1=whn[:], op=ALU.mult),
                [og[k], whn], [wgo[k]], eng="gpsimd")
            for hp in range(2):
                kk = 2 * k + hp
                pu = bank(C)
                for q in range(2):
                    hh = 2 * hp + q
                    mm(C, pu, pu[0:96, q * 256:q * 256 + 193], uk[k], uk[k][:, hh, :], s.vaug, s.vaug[:, b, hh, :])
                for q in range(2):
                    hh = 2 * hp + q
                    dcol = s.DEC[0:96, b * 4 + hh:b * 4 + hh + 1]
                    vec(C, "tensor_scalar", dict(out=Dbf[kk][:, q, :], in0=Cn[:, hh, :], scalar1=dcol, scalar2=None,
                                                 op0=ALU.mult), [Cn, s.DEC], [Dbf[kk]])
                    vec(C, "scalar_tensor_tensor", dict(
                        out=Cn[:, hh, :], in0=Cn[:, hh, :], scalar=dcol, in1=pu[0:96, q * 256:q * 256 + 193],
                        op0=ALU.mult, op1=ALU.add), [Cn, s.DEC, pu], [Cn])

        def G(b):
            blk = t * NB + b
            k = blk % 2
            cs = slice(b * 128, (b + 1) * 128)
            paccs = []
            for hp in range(2):
                kk = 2 * k + hp
                pacc = C.pb[6 + hp]
                paccs.append(pacc)
                for q in range(2):
                    hh = 2 * hp + q
                    mm(C, pacc, pacc[:, q * 256:q * 256 + 193], sT[k], sT[k][:, hh, :], s.vaug, s.vaug[:, b, hh, :],
                       start=True, stop=False)
                    mm(C, pacc, pacc[:, q * 256:q * 256 + 193], s.qkT, s.qkT[:, hh, cs], Dbf[kk], Dbf[kk][:, q, :],
                       start=False, stop=True)
            yield
            pvs = [p_[:, 0:512].rearrange("p (q d) -> p q d", q=2) for p_ in paccs]
            for hp in range(2):
                act(C, den[2 * k + hp], den[2 * k + hp][:], paccs[hp], pvs[hp][:, :, 192], AF.Abs)
            for hp in range(2):
                vec(C, "tensor_tensor", dict(out=den[2 * k + hp][:], in0=den[2 * k + hp][:],
                                             in1=s.TM[:, b * 8 + 4 + 2 * hp:b * 8 + 6 + 2 * hp], op=ALU.max),
                    [den[2 * k + hp], s.TM], [den[2 * k + hp]])
            yield
            for hp in range(2):
                vec(C, "reciprocal", dict(out=rcp[2 * k + hp][:], in_=den[2 * k + hp][:]), [den[2 * k + hp]], [rcp[2 * k + hp]])
            yield
            for hp in range(2):
                for q in range(2):
                    act(C, junk2, junk2[:], paccs[hp], paccs[hp][:, q * 256:q * 256 + 192], AF.Square,
                        scale=rcp[2 * k + hp][:, q:q + 1], accum=ss2[2 * k + hp][:, q:q + 1], rd=[rcp[2 * k + hp]], wr=[ss2[2 * k + hp]])
            yield
            for hp in range(2):
                act(C, t2[2 * k + hp], t2[2 * k + hp][:], ss2[2 * k + hp], ss2[2 * k + hp][:], AF.Ln, scale=1.0 / 192, bias=C.eps[:, 0:1], rd=[C.eps])
            for hp in range(2):
                act(C, rs2[2 * k + hp], rs2[2 * k + hp][:], t2[2 * k + hp], t2[2 * k + hp][:], AF.Exp, scale=-0.5)
            yield
            for hp in range(2):
                vec(C, "tensor_tensor", dict(out=rs2[2 * k + hp][:], in0=rs2[2 * k + hp][:], in1=rcp[2 * k + hp][:], op=ALU.mult),
                    [rs2[2 * k + hp], rcp[2 * k + hp]], [rs2[2 * k + hp]])
            yield
            for q in range(2):
                for hp in range(2):
                    hh = 2 * hp + q
                    vec(C, "scalar_tensor_tensor", dict(
                        out=yml[k][:, hh * 192:(hh + 1) * 192], in0=paccs[hp][:, q * 256:q * 256 + 192],
                        scalar=rs2[2 * k + hp][:, q:q + 1], in1=wgo[k][:, hh * 192:(hh + 1) * 192], op0=ALU.mult, op1=ALU.mult),
                        [paccs[hp], rs2[2 * k + hp], wgo[k]], [yml[k]])

        def H(b):
            blk = t * NB + b
            k = blk % 2
            cs = slice(b * 128, (b + 1) * 128)
            pty = bank(C)
            ptyv = pty.t[:, 0:512].bitcast(BF16)
            for c in range(6):
                tr(C, pty, ptyv[:, c * 128:(c + 1) * 128], yml[k], yml[k][:, c * 128:(c + 1) * 128], C.ident_b, C.ident_b[:])
            act(C, y, y[:, 0:6, cs], pty, ptyv[:, 0:768].rearrange("p (c t) -> p c t", c=6), AF.Copy)

        seq = []
        for b in range(NB):
            seq.append((F, b))
        for b in range(NB):
            seq.append((G, b))
            if b >= 1:
                seq.append((H, b - 1))
        seq.append((H, NB - 1))
        for fn, b in seq:
            r_ = fn(b)
            if r_ is not None:
                for _ in r_:
                    yield
            yield

    def stage3(t):
        y = yT[t % 2]

        def O(b):
            blk = t * NB + b
            k = blk % 2
            cs = slice(b * 128, (b + 1) * 128)
            r0 = blk * 128
            P.dma("sync", xres[k], xres[k][:], I["x"], I["x"][r0:r0 + 128, :])
            for half in range(2):
                pb = bank(C)
                for c in range(8):
                    mm(C, pb, pb[:, 0:512], y, y[:, c, cs], w_out, w_out[:, c, half * 512:(half + 1) * 512],
                       start=(c == 0), stop=(c == 7))
                vec(C, "tensor_tensor", dict(out=xo[k][:, half * 512:(half + 1) * 512], in0=pb[:, 0:512],
                                             in1=xres[k][:, half * 512:(half + 1) * 512], op=ALU.add),
                    [pb, xres[k]], [xo[k]])
            P.dma("gpsimd", C.X1, C.X1[r0:r0 + 128, :], xo[k], xo[k][:])

        mem_attn(C, qmT3[t % 3], y, T, eT, rec, cnt, pairs=(0,))
        yield
        mem_attn(C, qmT3[t % 3], y, T, eT, rec, cnt, pairs=(1,))
        yield
        for b in range(NB):
            O(b)
            yield

    def merge(gens):
        alive = list(gens)
        while alive:
            for g_, w_ in list(alive):
                try:
                    for _ in range(w_):
                        next(g_)
                except StopIteration:
                    alive.remove((g_, w_))

    for t in range(NT + 2):
        gens = []
        if t < NT:
            gens.append((stage1(t), 1))
        if 1 <= t <= NT:
            gens.append((stage2(t - 1), 2))
        if t >= 2:
            gens.append((stage3(t - 2), 1))
        merge(gens)
    C.nbank = 8
    barrier(P)
    pe_filler(C, None, on=False)
    A.release(m0)


def mem_attn(C, qmT, y, T, eT, rec, cnt, pairs=(0, 1)):
    for pair in pairs:
        psc0 = bank(C)
        pn = bank(C)
        pd = bank(C)
        n = 0
        for i in range(2):
            hh = 2 * pair + i
            for mc in range(2):
                psc = psc0 if n == 0 else bank(C)
                mm(C, psc, psc[:, 0:T], C.mem_kT, C.mem_kT[64 * i:64 * i + 64, pair, mc * 128:(mc + 1) * 128],
                   qmT, qmT[64 * i:64 * i + 64, pair, :])
                e = eT[cnt["e"] % 4]
                cnt["e"] += 1
                act(C, e, e[:], psc, psc[:, 0:T], AF.Exp, scale=0.125)
                mm(C, pn, pn[:, 0:T], C.vpad, C.vpad[:, mc, hh, :], e, e[:], start=(n == 0), stop=(n == 3))
                mm(C, pd, pd[:, 0:T], C.onespad, C.onespad[:, i, :], e, e[:], start=(n == 0), stop=(n == 3))
                n += 1
        r = rec[pair]
        act(C, r, r[:], pd, pd[:, 0:T], AF.Ln)
        act(C, r, r[:], r, r[:], AF.Exp, scale=-1.0)
        vec(C, "tensor_tensor", dict(out=y[:, 6 + pair, :], in0=pn[:, 0:T], in1=r[:],
                                                                op=ALU.mult), [pn, r], [y])

def ffn(C, Xin, Xout, gcol0, w1d, w2d, final=False, T=256, pre=None):
    P, A, I = C.P, C.A, C.I
    NB = T // 128
    NT = S // T
    m0 = A.mark()
    C.nbank = 7
    pe_filler(C, C.pb[7])
    W1 = pre[0] if pre is not None else load_w(C, "W1", w1d, 8, 0, DFF)
    W2 = load_w(C, "W2", w2d, 32, 0, D)
    hT = A.alloc("hT", [128, 8, T], BF16)
    uT = A.alloc("uT", [128, 32, T], BF16)
    xin = [A.alloc("xin%d" % i, [128, 1024], F32) for i in range(2)]
    xres = [A.alloc("xres%d" % i, [128, 1024], F32) for i in range(2)]
    xo = [A.alloc("xo%d" % i, [128, 1024], F32) for i in range(2)]
    rr = [A.alloc("rr%d" % i, [128, T], F32) for i in range(2)]
    junk = A.alloc("junk", [128, 1024], BF16)
    xn = A.alloc("xn", [128, 1024], BF16)
    ss = [A.alloc("ss%d" % i, [128, 1], F32) for i in range(2)]
    t1 = [A.alloc("t1%d" % i, [128, 1], F32) for i in range(2)]
    rstd = [A.alloc("rstd%d" % i, [128, 1], F32) for i in range(2)]
    if final:
        gfin = A.alloc("gfin", [128, 1024], F32)
        P.dma("sync", gfin, gfin[:], I["gfinal"], I["gfinal"][0:1, :].to_broadcast([128, 1024]))
    cnt = {"x": 0, "r": 0}

    def A1(t):
        for b in range(NB):
            k = cnt["x"] % 2
            cnt["x"] += 1
            r0 = t * T + b * 128
            P.dma("sync", xin[k], xin[k][:], Xin, Xin[r0:r0 + 128, :])
            norm_transpose(C, xin[k], xin[k][:], gcol0, hT, b * 128, (junk, ss[k], t1[k], rstd[k], xn))

    def A2(t):
        for f in range(32):
            pb = bank(C)
            for c in range(8):
                mm(C, pb, pb[:, 0:T], W1, W1[:, c, f * 128:(f + 1) * 128], hT, hT[:, c, :], start=(c == 0), stop=(c == 7))
            k = cnt["r"] % 2
            cnt["r"] += 1
            act(C, rr[k], rr[k][:], pb, pb[:, 0:T], AF.Relu)
            vec(C, "tensor_tensor", dict(out=uT[:, f, :], in0=rr[k][:], in1=rr[k][:], op=ALU.mult), [rr[k]], [uT],
                eng="gpsimd")

    def B(t):
        for b in range(NB):
            blk = t * NB + b
            k = blk % 2
            r0 = blk * 128
            cs = slice(b * 128, (b + 1) * 128)
            P.dma("sync", xres[k], xres[k][:], Xin, Xin[r0:r0 + 128, :])
            for half in range(2):
                pb = bank(C)
                for f in range(32):
                    mm(C, pb, pb[:, 0:512], uT, uT[:, f, cs], W2, W2[:, f, half * 512:(half + 1) * 512],
                       start=(f == 0), stop=(f == 31))
                vec(C, "tensor_tensor", dict(out=xo[k][:, half * 512:(half + 1) * 512], in0=pb[:, 0:512],
                                             in1=xres[k][:, half * 512:(half + 1) * 512], op=ALU.add),
                    [pb, xres[k]], [xo[k]])
            if final:
                act(C, junk, junk[:], xo[k], xo[k][:], AF.Square, accum=ss[k][:, 0:1], wr=[ss[k]])
                rms_rstd(C, (ss[k], ss[k][:, 0:1]), (t1[k], t1[k][:, 0:1]), (rstd[k], rstd[k][:, 0:1]), D)
                vec(C, "scalar_tensor_tensor", dict(out=xo[k][:], in0=xo[k][:], scalar=rstd[k][:, 0:1], in1=gfin[:],
                                                    op0=ALU.mult, op1=ALU.mult), [xo[k], rstd[k], gfin], [xo[k]])
            P.dma("gpsimd", Xout, Xout[r0:r0 + 128, :], xo[k], xo[k][:])

    A1(0)
    A2(0)
    for t in range(NT):
        if t + 1 < NT:
            A1(t + 1)
        B(t)
        if t + 1 < NT:
            A2(t + 1)
    C.nbank = 8
    barrier(P)
    pe_filler(C, None, on=False)
    A.release(m0)


CQ0, CKV0, KR0, QM1 = 0, 384, 640, 672


def feat_rmsnorm(C, pbs, nch, gcol, outT, T, sq, rbc, ones_f):
    pst = bank(C)
    for j in range(nch):
        q = sq[j % 2]
        act(C, q, q[:], pbs[j], pbs[j][:, 0:T], AF.Square)
        mm(C, pst, pst[:, 0:T], ones_f, ones_f[:], q, q[:], start=(j == 0), stop=(j == nch - 1))
    act(C, rbc, rbc[:], pst, pst[:, 0:T], AF.Ln, scale=1.0 / (nch * 128), bias=C.eps[:, 0:1], rd=[C.eps])
    act(C, rbc, rbc[:], rbc, rbc[:], AF.Exp, scale=-0.5)
    for j in range(nch):
        vec(C, "scalar_tensor_tensor", dict(out=outT[:, j, :], in0=pbs[j][:, 0:T], scalar=C.smalls[:, gcol + j:gcol + j + 1],
                                            in1=rbc[:], op0=ALU.mult, op1=ALU.mult), [pbs[j], C.smalls, rbc], [outT])


def l1proj(C, T=512):
    P, A, I = C.P, C.A, C.I
    NB = T // 128
    NT = S // T
    m0 = A.mark()
    C.nbank = 7
    pe_filler(C, C.pb[7])
    w_in = load_w(C, "w_in1", I["w_in1"], 8, 0, 928)
    w_krB = load_w(C, "w_krB", I["w_krB"], 8, 0, 32)
    w_qn = load_w(C, "w_uq_nope", I["w_uq_nope"], 3, 0, 768)
    w_qA = load_w(C, "w_uq_rA", I["w_uq_rA"], 3, 0, 384)
    w_qB = load_w(C, "w_uq_rB", I["w_uq_rB"], 3, 0, 384)
    w_uk = load_w(C, "w_uk", I["w_uk"], 2, 0, 768)
    w_uv = load_w(C, "w_uv", I["w_uv"], 2, 0, 768)
    ones_f = A.alloc("ones_f", [128, 128], F32)
    vec(C, "memset", dict(ap=ones_f[:], constant=1.0), [], [ones_f])
    hT = A.alloc("hT", [128, 8, T], BF16)
    xin = [A.alloc("xin%d" % i, [128, 1024], F32) for i in range(2)]
    junk = A.alloc("junk", [128, 1024], BF16)
    xn = A.alloc("xn", [128, 1024], BF16)
    ss = [A.alloc("ss%d" % i, [128, 1], F32) for i in range(2)]
    t1 = [A.alloc("t1%d" % i, [128, 1], F32) for i in range(2)]
    rstd = [A.alloc("rstd%d" % i, [128, 1], F32) for i in range(2)]
    cosT = [A.alloc("cosT%d" % i, [128, T], F32) for i in range(2)]
    sinT = [A.alloc("sinT%d" % i, [128, T], F32) for i in range(2)]
    sq = [A.alloc("sq%d" % i, [128, T], F32) for i in range(2)]
    rbc = [A.alloc("rbc%d" % i, [128, T], F32) for i in range(2)]
    cqn = [A.alloc("cqn%d" % i, [128, 3, T], BF16) for i in range(2)]
    ckvn = [A.alloc("ckvn%d" % i, [128, 2, T], BF16) for i in range(2)]
    ta = [A.alloc("ta%d" % i, [128, T], F32) for i in range(2)]
    tb = [A.alloc("tb%d" % i, [128, T], F32) for i in range(2)]
    stg = [A.alloc("stg%d" % i, [128, T], BF16) for i in range(12)]
    qmT = [A.alloc("qmT%d" % i, [128, 2, T], BF16) for i in range(2)]
    ymT = [A.alloc("ymT%d" % i, [128, 8, T], BF16) for i in range(1)]
    eT = [A.alloc("eT%d" % i, [128, T], BF16) for i in range(4)]
    rec = [A.alloc("rec%d" % i, [128, T], F32) for i in range(2)]
    vtok = [A.alloc("vtok%d" % i, [128, 768], BF16) for i in range(4)]
    cnt = {"x": 0, "e": 0, "s": 0, "r": 0, "v": 0}

    def A1(t):
        for b in range(NB):
            k = cnt["x"] % 2
            cnt["x"] += 1
            r0 = t * T + b * 128
            P.dma("sync", xin[k], xin[k][:], C.X2, C.X2[r0:r0 + 128, :])
            norm_transpose(C, xin[k], xin[k][:], 16, hT, b * 128, (junk, ss[k], t1[k], rstd[k], xn))

    def stage(nm):
        b_ = stg[cnt["s"] % 12]
        cnt["s"] += 1
        return b_

    def rope(pA, pB, np_, out_ap, out_buf, cs_, sn_):
        k = cnt["r"] % 2
        cnt["r"] += 1
        vec(C, "tensor_tensor", dict(out=ta[k][0:np_, :], in0=pA[0:np_, 0:T], in1=cs_[0:np_, :], op=ALU.mult),
            [pA, cs_], [ta[k]])
        vec(C, "tensor_tensor", dict(out=tb[k][0:np_, :], in0=pB[0:np_, 0:T], in1=sn_[0:np_, :], op=ALU.mult),
            [pB, sn_], [tb[k]])
        vec(C, "tensor_tensor", dict(out=out_ap, in0=ta[k][0:np_, :], in1=tb[k][0:np_, :], op=ALU.add),
            [ta[k], tb[k]], [out_buf], eng="gpsimd")

    def X(t):
        cols = slice(t * T, (t + 1) * T)
        cs_, sn_ = cosT[t % 2], sinT[t % 2]
        cq_, ckv_, qm_ = cqn[t % 2], ckvn[t % 2], qmT[t % 2]
        P.dma("sync", cs_, cs_[:], C.ROPEC, C.ROPEC[:, cols])
        P.dma("sync", sn_, sn_[:], C.ROPES, C.ROPES[:, cols])
        pcq = []
        for j in range(3):
            pb = bank(C)
            for c in range(8):
                mm(C, pb, pb[:, 0:T], w_in, w_in[:, c, CQ0 + j * 128:CQ0 + (j + 1) * 128], hT, hT[:, c, :],
                   start=(c == 0), stop=(c == 7))
            pcq.append(pb)
        pkv = []
        for j in range(2):
            pb = bank(C)
            for c in range(8):
                mm(C, pb, pb[:, 0:T], w_in, w_in[:, c, CKV0 + j * 128:CKV0 + (j + 1) * 128], hT, hT[:, c, :],
                   start=(c == 0), stop=(c == 7))
            pkv.append(pb)
        feat_rmsnorm(C, pcq, 3, 0, cq_, T, sq, rbc[0], ones_f)
        pA = bank(C)
        for c in range(8):
            mm(C, pA, pA[0:32, 0:T], w_in, w_in[:, c, KR0:KR0 + 32], hT, hT[:, c, :], start=(c == 0), stop=(c == 7))
        pB = bank(C)
        for c in range(8):
            mm(C, pB, pB[0:32, 0:T], w_krB, w_krB[:, c, :], hT, hT[:, c, :], start=(c == 0), stop=(c == 7))
        feat_rmsnorm(C, pkv, 2, 3, ckv_, T, sq, rbc[1], ones_f)
        st = stage("kr")
        rope(pA, pB, 32, st[0:32, :], st, cs_, sn_)
        P.dma("sync", C.KR, C.KR[:, cols], st, st[0:32, :])
        yield
        for pair in range(2):
            pb = bank(C)
            for c in range(8):
                mm(C, pb, pb[:, 0:T], w_in, w_in[:, c, QM1 + pair * 128:QM1 + (pair + 1) * 128], hT, hT[:, c, :],
                   start=(c == 0), stop=(c == 7))
            act(C, qm_, qm_[:, pair, :], pb, pb[:, 0:T], AF.Copy)
        yield
        if t + 1 < NT:
            for _ in A1g(t + 1):
                yield

    def A1g(t):
        for b in range(NB):
            k = cnt["x"] % 2
            cnt["x"] += 1
            r0 = t * T + b * 128
            P.dma("sync", xin[k], xin[k][:], C.X2, C.X2[r0:r0 + 128, :])
            norm_transpose(C, xin[k], xin[k][:], 16, hT, b * 128, (junk, ss[k], t1[k], rstd[k], xn))
            yield

    def Y(t):
        cols = slice(t * T, (t + 1) * T)
        cs_, sn_ = cosT[t % 2], sinT[t % 2]
        cq_, ckv_, qm_ = cqn[t % 2], ckvn[t % 2], qmT[t % 2]
        ym = ymT[0]
        mem_attn(C, qm_, ym, T, eT, rec, cnt)
        for pair in range(2):
            P.dma("gpsimd", C.YT, C.YT[768 + pair * 128:768 + (pair + 1) * 128, cols], ym, ym[:, 6 + pair, :])
        yield
        for g in range(6):
            pb = bank(C)
            for j in range(3):
                mm(C, pb, pb[:, 0:T], w_qn, w_qn[:, j, g * 128:(g + 1) * 128], cq_, cq_[:, j, :], start=(j == 0), stop=(j == 2))
            st = stage("qn")
            act(C, st, st[:], pb, pb[:, 0:T], AF.Copy)
            for i in range(2):
                P.dma("gpsimd", C.QT, C.QT[2 * g + i, 0:64, cols], st, st[64 * i:64 * i + 64, :])
            if g % 2 == 1:
                yield
        for g in range(3):
            pA = bank(C)
            for j in range(3):
                mm(C, pA, pA[:, 0:T], w_qA, w_qA[:, j, g * 128:(g + 1) * 128], cq_, cq_[:, j, :], start=(j == 0), stop=(j == 2))
            pB = bank(C)
            for j in range(3):
                mm(C, pB, pB[:, 0:T], w_qB, w_qB[:, j, g * 128:(g + 1) * 128], cq_, cq_[:, j, :], start=(j == 0), stop=(j == 2))
            st = stage("qr")
            rope(pA, pB, 128, st[:], st, cs_, sn_)
            for i in range(4):
                P.dma("gpsimd", C.QT, C.QT[4 * g + i, 64:96, cols], st, st[32 * i:32 * i + 32, :])
            yield
        for g in range(6):
            pb = bank(C)
            for j in range(2):
                mm(C, pb, pb[:, 0:T], w_uk, w_uk[:, j, g * 128:(g + 1) * 128], ckv_, ckv_[:, j, :], start=(j == 0), stop=(j == 1))
            st = stage("kn")
            act(C, st, st[:], pb, pb[:, 0:T], AF.Copy)
            for i in range(2):
                P.dma("gpsimd", C.KT, C.KT[2 * g + i, :, cols], st, st[64 * i:64 * i + 64, :])
            if g % 2 == 1:
                yield
        for b in range(NB):
            vk = vtok[cnt["v"] % 4]
            cnt["v"] += 1
            for half in range(2):
                pb = bank(C)
                for j in range(2):
                    mm(C, pb, pb[:, 0:384], ckv_, ckv_[:, j, b * 128:(b + 1) * 128], w_uv,
                       w_uv[:, j, half * 384:(half + 1) * 384], start=(j == 0), stop=(j == 1))
                act(C, vk, vk[:, half * 384:(half + 1) * 384], pb, pb[:, 0:384], AF.Copy)
            r0 = t * T + b * 128
            P.dma("gpsimd", C.VV, C.VV[r0:r0 + 128, :], vk, vk[:])
            if b % 2 == 1:
                yield

    def merge(gens):
        alive = list(gens)
        while alive:
            for g_ in list(alive):
                try:
                    next(g_)
                except StopIteration:
                    alive.remove(g_)

    for _ in A1g(0):
        pass
    for t in range(NT + 1):
        gens = []
        if t < NT:
            gens.append(X(t))
        if t >= 1:
            gens.append(Y(t - 1))
        merge(gens)
    C.nbank = 8
    barrier(P)
    pe_filler(C, None, on=False)
    A.release(m0)


def l1attn(C, T=512):
    P, A, I = C.P, C.A, C.I
    NQ = S // T
    m0 = A.mark()
    pe_filler(C, C.pb[2])
    HB = []
    for i in range(2):
        d = Ctx()
        d.K = A.alloc("Kh%d" % i, [96, S], BF16)
        d.Q = A.alloc("Qh%d" % i, [96, S], BF16)
        d.V = A.alloc("Vh%d" % i, [128, NBLK, 65], BF16)
        vec(C, "memset", dict(ap=d.V[:], constant=1.0), [], [d.V])
        HB.append(d)
    onesrow = A.alloc("onesrow", [128, 64], F32)
    vec(C, "memset", dict(ap=onesrow[:], constant=1.0), [], [onesrow])
    eT = [A.alloc("eT%d" % i, [128, 2 * T], BF16) for i in range(4)]
    rrow = [A.alloc("rrow%d" % i, [128, T], F32) for i in range(5)]
    nsb = [A.alloc("nsb%d" % i, [64, T], F32) for i in range(5)]
    yh = [A.alloc("yh%d" % i, [64, T], BF16) for i in range(5)]
    scale = 96.0 ** -0.5
    ne = 0
    nq = 0
    pair_banks = [C.pbp[2], C.pbp[3]]
    npsc = 0

    def load_head(hh):
        d = HB[hh % 2]
        P.dma("sync", d.K, d.K[0:64, :], C.KT, C.KT[hh, :, :])
        P.dma("sync", d.K, d.K[64:96, :], C.KR, C.KR[:, :])
        P.dma("sync", d.Q, d.Q[:, :], C.QT, C.QT[hh, :, :])
        vsrc = C.VV[:, hh * 64:(hh + 1) * 64].rearrange("(b p) d -> p b d", p=128)
        for q4 in range(4):
            P.dma("sync", d.V, d.V[:, q4 * 8:(q4 + 1) * 8, 0:64], C.VV, vsrc[:, q4 * 8:(q4 + 1) * 8, :])

    rhl = [A.alloc("rhl%d" % i, [128, 2, T], BF16) for i in range(5)]
    onesb = A.alloc("onesb", [128, 64], BF16)
    vec(C, "memset", dict(ap=onesb[:], constant=1.0), [], [onesb])
    LAG = 3

    items = []
    for hh in range(12):
        for qt in range(NQ):
            nkb = 4 * qt + 4
            for kb in range(nkb):
                items.append((hh, qt, kb, nkb))

    state = {"npsc": 0, "ne": 0, "pe_ns": 0.0}

    def frontp(ia, ib):
        hh, qt, kba, nkb = ia
        kbb = ib[2]
        d = HB[hh % 2]
        psc = pair_banks[state["npsc"] % 2]
        state["npsc"] += 1
        e = eT[state["ne"] % 4]
        state["ne"] += 1
        c0s = []
        for half, kb in ((0, kba), (1, kbb)):
            j = kb - 4 * qt
            c0 = 128 * j if j > 0 else 0
            c0s.append(c0)
            mm(C, psc, psc[:, half * T + c0:(half + 1) * T], d.K, d.K[:, kb * 128:(kb + 1) * 128],
               d.Q, d.Q[:, qt * T + c0:(qt + 1) * T])
        lo = c0s[0]
        act(C, e, e[:, lo:2 * T], psc, psc[:, lo:2 * T], AF.Exp, scale=scale)
        for half, kb in ((0, kba), (1, kbb)):
            j = kb - 4 * qt
            if j >= 0:
                a0 = half * T + c0s[half]
                vec(C, "tensor_tensor", dict(out=e[:, a0:a0 + 128], in0=e[:, a0:a0 + 128], in1=C.tri_b[:], op=ALU.mult),
                    [e, C.tri_b], [e], eng="gpsimd")
        return [(c0s[0], e, 0), (c0s[1], e, 1)]

    def back(it, fr):
        hh, qt, kb, nkb = it
        c0, e, half = fr
        d = HB[hh % 2]
        qi = hh * NQ + qt
        pa = C.pb[qi % 2]
        k2 = qi % 5
        if qt == 0 and kb == 0 and hh + 1 < 12:
            load_head(hh + 1)
        mm(C, pa, pa[0:65, c0:T], d.V, d.V[:, kb, :], e, e[:, half * T + c0:(half + 1) * T], start=(kb == 0), stop=(kb == nkb - 1))
        state["pe_ns"] += 2 * (T - c0) / 2.4
        if kb == nkb - 1:
            rr = rrow[k2]
            vec(C, "tensor_copy", dict(out=rr[64:65, :], in_=pa[64:65, 0:T]), [pa], [rr])
            vec(C, "tensor_copy", dict(out=nsb[k2][:], in_=pa[0:64, 0:T]), [pa], [nsb[k2]])
            vec(C, "reciprocal", dict(out=rr[64:65, :], in_=rr[64:65, :]), [rr], [rr])
            vec(C, "tensor_copy", dict(out=rhl[k2][64:65, 0, :], in_=rr[64:65, :]), [rr], [rhl[k2]])
            vec(C, "tensor_tensor", dict(out=rhl[k2][64:65, 1, :], in0=rr[64:65, :], in1=rhl[k2][64:65, 0, :],
                                         op=ALU.subtract), [rr, rhl[k2]], [rhl[k2]])

            def norm2(k2=k2, hh=hh, qt=qt):
                pbc = C.pb[3]
                mm(C, pbc, pbc[0:64, 0:T], onesb, onesb[64:65, :], rhl[k2], rhl[k2][64:65, 0, :], start=True, stop=False)
                mm(C, pbc, pbc[0:64, 0:T], onesb, onesb[64:65, :], rhl[k2], rhl[k2][64:65, 1, :], start=False, stop=True)
                vec(C, "tensor_tensor", dict(out=yh[k2][:], in0=nsb[k2][:], in1=pbc[0:64, 0:T], op=ALU.mult),
                    [nsb[k2], pbc], [yh[k2]])
                P.dma("sync", C.YT, C.YT[hh * 64:(hh + 1) * 64, qt * T:(qt + 1) * T], yh[k2], yh[k2][:])
            return norm2
        return None

    load_head(0)
    pend = []
    LAGP = 2
    deferred = []

    def do_back(a, b):
        n2 = back(a, b)
        if n2 is not None:
            deferred.append([state["pe_ns"] + 8000.0, n2])
        while deferred and (deferred[0][0] <= state["pe_ns"] or len(deferred) >= 4):
            deferred.pop(0)[1]()

    for i in range(0, len(items), 2):
        ia, ib = items[i], items[i + 1]
        fr = frontp(ia, ib)
        pend.append(((ia, fr[0]), (ib, fr[1])))
        if len(pend) > LAGP:
            pa_, pb_ = pend.pop(0)
            do_back(*pa_)
            do_back(*pb_)
    while pend:
        pa_, pb_ = pend.pop(0)
        do_back(*pa_)
        do_back(*pb_)
    for dfr in deferred:
        dfr[1]()
    barrier(P)
    pe_filler(C, None, on=False)
    A.release(m0)


def l1out(C, T=512):
    P, A, I = C.P, C.A, C.I
    NB = T // 128
    NT = S // T
    m0 = A.mark()
    C.nbank = 7
    pe_filler(C, C.pb[7])
    w_out = load_w(C, "w_out1", I["w_out1"], 8, 0, 1024)
    yT = [A.alloc("yT%d" % i, [128, 8, T], BF16) for i in range(3)]
    xres = [A.alloc("xres%d" % i, [128, 1024], F32) for i in range(8)]
    xo = [A.alloc("xo%d" % i, [128, 1024], F32) for i in range(4)]
    ysrc = C.YT.t.rearrange("(c p) t -> p c t", p=128)

    def loads(t):
        y = yT[t % 3]
        P.dma("sync", y, y[:], C.YT, ysrc[:, :, t * T:(t + 1) * T])
        for b in range(NB):
            blk = t * NB + b
            P.dma("sync", xres[blk % 8], xres[blk % 8][:], C.X2, C.X2[blk * 128:(blk + 1) * 128, :])

    loads(0)
    for t in range(NT):
        if t + 1 < NT:
            loads(t + 1)
        y = yT[t % 3]
        for b in range(NB):
            blk = t * NB + b
            k = blk % 4
            r0 = blk * 128
            cs = slice(b * 128, (b + 1) * 128)
            for half in range(2):
                pb = bank(C)
                for c in range(8):
                    mm(C, pb, pb[:, 0:512], y, y[:, c, cs], w_out, w_out[:, c, half * 512:(half + 1) * 512],
                       start=(c == 0), stop=(c == 7))
                vec(C, "tensor_tensor", dict(out=xo[k][:, half * 512:(half + 1) * 512], in0=pb[:, 0:512],
                                             in1=xres[blk % 8][:, half * 512:(half + 1) * 512], op=ALU.add),
                    [pb, xres[blk % 8]], [xo[k]])
            P.dma("gpsimd", C.X3, C.X3[r0:r0 + 128, :], xo[k], xo[k][:])
    C.nbank = 8
    barrier(P)
    pe_filler(C, None, on=False)
    A.release(m0)

import numpy as np

ARENA_BYTES = 200 * 1024

INPUT_SPECS = [
    ("x", [S, D], F32), ("mem", [256, D], F32), ("pos", [1, S], I32),
    ("w_mem_kv", [1024, 512], F32), ("w_in0", [1024, 2568], F32), ("w_out0", [1024, 1024], F32),
    ("w_ff1_0", [1024, 4096], F32), ("w_ff2_0", [4096, 1024], F32),
    ("w_in1", [1024, 928], F32), ("w_krB", [1024, 32], F32),
    ("w_uq_nope", [384, 768], F32), ("w_uq_rA", [384, 384], F32), ("w_uq_rB", [384, 384], F32),
    ("w_uk", [256, 768], F32), ("w_uv", [256, 768], F32), ("w_out1", [1024, 1024], F32),
    ("w_ff1_1", [1024, 4096], F32), ("w_ff2_1", [4096, 1024], F32),
    ("gains", [128, 40], F32), ("smalls", [128, 8], F32), ("convT", [96, 32], F32), ("bif", [4, 2], F32),
    ("whn", [1, 768], F32), ("gfinal", [1, 1024], F32), ("ident", [128, 128], F32), ("tri", [128, 128], F32),
]


def build(upto=99, dbg=(), only=None):
    nc = bass.Bass("TRN2", target_bir_lowering=False)
    import os
    P = Prog(nc, strict_same=tuple(x for x in os.environ.get('STRICT', 'vector,scalar,gpsimd').split(',') if x))
    C = Ctx()
    C.P = P
    C.I = {}
    for name, shape, dt in INPUT_SPECS:
        C.I[name] = P.dram(name, shape, dt, kind="ExternalInput")
    C.out = P.dram("out", [S, D], F32, kind="ExternalOutput")
    C.out.multi = True

    def scratch(name, shape, dt):
        b = P.dram(name, shape, dt, kind=("ExternalOutput" if name in dbg else "Internal"))
        b.multi = True
        return b

    C.ROPEC = scratch("ROPEC", [128, S], F32)
    C.ROPES = scratch("ROPES", [128, S], F32)
    C.X1 = scratch("X1", [S, D], F32)
    C.X2 = scratch("X2", [S, D], F32)
    C.X3 = scratch("X3", [S, D], F32)
    C.QT = scratch("QT", [12, 96, S], BF16)
    C.KT = scratch("KT", [12, 64, S], BF16)
    C.KR = scratch("KR", [32, S], BF16)
    C.VV = scratch("VV", [S, 768], BF16)
    C.YT = scratch("YT", [1024, S], BF16)
    C.A = Arena(P, ARENA_BYTES)
    C.pbp = [P.psum("pp%d" % j, [128, 1024], F32) for j in range(4)]
    C.pb = []
    for j in range(4):
        for h_ in range(2):
            C.pb.append(Buf(C.pbp[j].t[:, 512 * h_:512 * (h_ + 1)], "pb%d" % (2 * j + h_)))
    C.bi = 0
    on = (lambda i: upto >= i) if only is None else (lambda i: i in only)
    C.WB = {}
    C.bgcast = False
    phase0(C)
    if on(1):
        l0mix(C)
    if on(2):
        ffn(C, C.X1, C.X2, 8, C.I["w_ff1_0"], C.I["w_ff2_0"])
    if on(3):
        l1proj(C)
    pre = None
    if on(6):
        pre = (load_w(C, "W1p", C.I["w_ff1_1"], 8, 0, DFF), None)
    if on(4):
        l1attn(C)
    if on(5):
        l1out(C)
    if on(6):
        ffn(C, C.X3, C.out, 24, C.I["w_ff1_1"], C.I["w_ff2_1"], final=True, pre=pre)
    fin = [C.out] if upto >= 99 else []
    for nm in dbg:
        fin.append(getattr(C, nm))
    P.finish("sync", fin)
    P.emit()
    return nc, C


def prep_inputs(inp, b):
    f = lambda a: np.ascontiguousarray(a, dtype=np.float32)
    m = {}
    m["x"] = f(inp["x"][b])
    m["mem"] = f(inp["mem"][b])
    m["pos"] = np.ascontiguousarray(inp["positions"][b].reshape(1, S).astype(np.int32))
    for k in ("w_mem_kv", "w_in0", "w_out0", "w_ff1_0", "w_ff2_0", "w_in1", "w_out1", "w_ff1_1", "w_ff2_1"):
        m[k] = f(inp[k])
    w_in1 = inp["w_in1"]
    kr = w_in1[:, 640:672]
    m["w_krB"] = f(np.concatenate([kr[:, 16:32], kr[:, 0:16]], axis=1))
    wuq = inp["w_uq1"].reshape(384, 12, 96)
    m["w_uq_nope"] = f(wuq[:, :, 0:64].reshape(384, 768))
    m["w_uq_rA"] = f(wuq[:, :, 64:96].reshape(384, 384))
    m["w_uq_rB"] = f(np.concatenate([wuq[:, :, 80:96], wuq[:, :, 64:80]], axis=2).reshape(384, 384))
    wukv = inp["w_ukv1"].reshape(256, 12, 128)
    m["w_uk"] = f(wukv[:, :, 0:64].reshape(256, 768))
    m["w_uv"] = f(wukv[:, :, 64:128].reshape(256, 768))
    gains = np.zeros((128, 40), np.float32)
    for i, k in enumerate(("norm_mix0", "norm_ffn0", "norm_mix1", "norm_ffn1", "mem_norm")):
        gains[:, i * 8:(i + 1) * 8] = inp[k].reshape(8, 128).T
    m["gains"] = gains
    sm = np.zeros((128, 8), np.float32)
    sm[:, 0:3] = inp["w_qnorm1"].reshape(3, 128).T
    sm[:, 3:5] = inp["w_kvnorm1"].reshape(2, 128).T
    inv = (10000.0 ** (-np.arange(0, 32, 2, dtype=np.float32) / 32)).astype(np.float32)
    p = np.arange(128)
    sm[:, 5] = inv[p % 16]
    sm[:, 6] = np.where((p % 32) < 16, -1.0, 1.0)
    m["smalls"] = sm
    wc = inp["w_conv0"]
    m["convT"] = f(wc.reshape(4, 8, 96).transpose(2, 1, 0).reshape(96, 32))
    m["bif"] = f(np.stack([inp["b_igate0"], inp["b_fgate0"]], axis=1))
    m["whn"] = f(inp["w_hnorm0"].reshape(1, 768))
    m["gfinal"] = f(inp["final_norm"].reshape(1, 1024))
    m["ident"] = np.eye(128, dtype=np.float32)
    m["tri"] = np.triu(np.ones((128, 128), np.float32))
    return m

from concourse.bass_utils import run_bass_kernel_spmd

_NC = None


def kernel(**inputs):
    global _NC
    inp = {k: np.asarray(v) for k, v in inputs.items()}
    if _NC is None:
        _NC = build(upto=99)[0]
    nb = inp["x"].shape[0]
    in_maps = [prep_inputs(inp, b) for b in range(nb)]
    res = run_bass_kernel_spmd(_NC, in_maps, core_ids=list(range(nb)))
    return np.stack([np.asarray(r["out"], dtype=np.float32) for r in res.results], axis=0)
```

```python
import numpy as np
from contextlib import ExitStack
import concourse.bass as bass
import concourse.mybir as mybir

F32 = mybir.dt.float32
BF16 = mybir.dt.bfloat16
I32 = mybir.dt.int32
AF = mybir.ActivationFunctionType
ALU = mybir.AluOpType
AX = mybir.AxisListType

ENGS = ("tensor", "vector", "scalar", "gpsimd", "sync")
SEM_ROLL = 30000


class Tok:
    __slots__ = ("sem", "val", "dma")

    def __init__(self, sem, val, dma=False):
        self.sem = sem
        self.val = val
        self.dma = dma


class Buf:
    def __init__(self, t, name=""):
        self.t = t
        self.name = name
        self.w = {}
        self.r = {}
        self.dsem = None
        self.is_dram = False
        self.multi = False

    def __getitem__(self, idx):
        return self.t[idx]


class DSem:
    def __init__(self, sem):
        self.sem = sem
        self.total = 0
        self.open = False


class Prog:
    def __init__(self, nc, strict_same=("vector", "scalar", "gpsimd")):
        self.nc = nc
        self.stack = ExitStack()
        self.ops = {e: [] for e in ENGS}
        self.sem = {}
        self.cnt = {}
        self.seen = {e: {} for e in ENGS}
        self.strict = set(strict_same)
        self.nsem = 0
        self.dsems = {}
        for e in ("tensor", "vector", "scalar", "gpsimd"):
            self.sem[e] = self._new_sem("c_" + e)
            self.cnt[e] = 0
        self.nops = 0
        import os
        self.same_waw = os.environ.get('SAME_WAW', '0') == '1'
        self.rec = []
        self.sched = True
        self.nfill = 0

    def _new_sem(self, name):
        self.nsem += 1
        return self.stack.enter_context(self.nc.semaphore(name + "_%d" % self.nsem))

    def sbuf(self, name, shape, dt):
        t = self.stack.enter_context(self.nc.sbuf_tensor(name, list(shape), dt))
        return Buf(t, name)

    def psum(self, name, shape, dt):
        t = self.stack.enter_context(self.nc.psum_tensor(name, list(shape), dt))
        return Buf(t, name)

    def dram(self, name, shape, dt, kind="Internal"):
        t = self.nc.dram_tensor(name, list(shape), dt, kind=kind)
        b = Buf(t.ap(), name)
        b.is_dram = True
        return b

    def view(self, buf, name=""):
        return Buf(buf.t, name or buf.name)

    def _collect(self, eng, reads, writes):
        need = {}
        own = self.sem.get(eng)

        def add(t, raw):
            if t is None:
                return
            if t.sem is own and not raw and not self.same_waw:
                return
            k = id(t.sem)
            if k not in need or need[k].val < t.val:
                need[k] = t

        for b in reads:
            for t in b.w.values():
                add(t, True)
        for b in writes:
            if b.multi:
                continue
            for t in b.w.values():
                add(t, False)
            for t in b.r.values():
                add(t, False)
        out = []
        for t in need.values():
            if t.sem is own and eng not in self.strict:
                continue
            val = t.val
            if t.dma:
                ds = self.dsems[id(t.sem)]
                val = ds.total
                ds.open = False
            if self.seen[eng].get(id(t.sem), 0) >= val:
                continue
            self.seen[eng][id(t.sem)] = val
            out.append((t.sem, val))
        return out

    def _mark(self, tok, reads, writes):
        for b in reads:
            b.r[id(tok.sem)] = tok
        for b in writes:
            if b.multi:
                b.w[id(tok.sem)] = tok
            else:
                b.w = {id(tok.sem): tok}
                b.r = {}

    def _op(self, eng, fn, reads=(), writes=()):
        waits = self._collect(eng, reads, writes)
        if self.cnt[eng] >= SEM_ROLL:
            self.sem[eng] = self._new_sem("c_" + eng)
            self.cnt[eng] = 0
        self.cnt[eng] += 1
        tok = Tok(self.sem[eng], self.cnt[eng])

        def run(h, waits=waits, fn=fn, tok=tok):
            for s, v in waits:
                h.wait_ge(s, v)
            fn(h).then_inc(tok.sem, 1)

        self.ops[eng].append(run)
        self._mark(tok, reads, writes)
        self.nops += 1
        return tok

    def _dma(self, eng, out_buf, out_ap, in_buf, in_ap, owner=None, **kw):
        if owner is None:
            owner = in_buf if (out_buf.is_dram and not in_buf.is_dram) else out_buf
        if owner.dsem is None:
            owner.dsem = DSem(self._new_sem("d_" + owner.name))
            self.dsems[id(owner.dsem.sem)] = owner.dsem
        ds = owner.dsem
        waits = self._collect(eng, [in_buf], [out_buf])
        if not ds.open and ds.total > 0:
            if self.seen[eng].get(id(ds.sem), 0) < ds.total:
                self.seen[eng][id(ds.sem)] = ds.total
                waits.append((ds.sem, ds.total))
        ds.total += 16
        ds.open = True
        tok = Tok(ds.sem, ds.total, dma=True)

        def run(h, waits=waits, tok=tok, out_ap=out_ap, in_ap=in_ap, kw=kw):
            for s, v in waits:
                h.wait_ge(s, v)
            h.dma_start(out=out_ap, in_=in_ap, **kw).then_inc(tok.sem, 16)

        self.ops[eng].append(run)
        self._mark(tok, [in_buf], [out_buf])
        self.nops += 1
        return tok

    def _finish(self, eng, bufs):
        waits = self._collect(eng, bufs, [])

        def run(h, waits=waits):
            for s, v in waits:
                h.wait_ge(s, v)

        self.ops[eng].append(run)

    def _emit(self):
        nc = self.nc
        ops = self.ops
        with nc.Block() as block:

            @block.tensor
            def _(h):
                for f in ops["tensor"]:
                    f(h)

            @block.vector
            def _(h):
                for f in ops["vector"]:
                    f(h)

            @block.scalar
            def _(h):
                for f in ops["scalar"]:
                    f(h)

            @block.gpsimd
            def _(h):
                for f in ops["gpsimd"]:
                    f(h)

            @block.sync
            def _(h):
                for f in ops["sync"]:
                    f(h)

        self.stack.close()

    def op(self, eng, fn, reads=(), writes=(), est=300.0):
        self.rec.append(("op", eng, fn, list(reads), list(writes), float(est)))

    def dma(self, eng, out_buf, out_ap, in_buf, in_ap, owner=None, est=None, **kw):
        if est is None:
            try:
                nb = 1
                for d_ in out_ap.shape:
                    nb *= int(d_)
                nb *= 2 if out_ap.dtype == BF16 else 4
            except Exception:
                nb = 65536
            est = 2000.0 + nb / 150.0
        self.rec.append(("dma", eng, (out_buf, out_ap, in_buf, in_ap, owner, kw), [in_buf], [out_buf], float(est)))

    def finish(self, eng, bufs):
        self.rec.append(("finish", eng, list(bufs)))

    def barrier(self):
        self.rec.append(("barrier",))

    def set_filler(self, spec):
        self.rec.append(("filler", spec))

    def _barrier(self):
        toks = []
        for e in ("tensor", "vector", "scalar", "gpsimd"):
            if self.cnt[e] > 0:
                toks.append((self.sem[e], self.cnt[e], e))
        dtoks = []
        for ds in self.dsems.values():
            if ds.total > 0:
                dtoks.append((ds.sem, ds.total))
                ds.open = False
        for eng in ENGS:
            waits = []
            for s_, v, e in toks:
                if e == eng:
                    continue
                if self.seen[eng].get(id(s_), 0) >= v:
                    continue
                self.seen[eng][id(s_)] = v
                waits.append((s_, v))
            for s_, v in dtoks:
                if self.seen[eng].get(id(s_), 0) >= v:
                    continue
                self.seen[eng][id(s_)] = v
                waits.append((s_, v))

            def run(h, waits=waits):
                for s2, v2 in waits:
                    h.wait_ge(s2, v2)

            self.ops[eng].append(run)

    def _schedule(self, seg):
        import heapq
        n = len(seg)
        preds = [set() for _ in range(n)]
        lastw, readers = {}, {}
        for i, it in enumerate(seg):
            reads, writes = it[3], it[4]
            for b in reads:
                preds[i].update(lastw.get(id(b), ()))
            for b in writes:
                if b.multi:
                    continue
                preds[i].update(lastw.get(id(b), ()))
                preds[i].update(readers.get(id(b), ()))
            for b in reads:
                readers.setdefault(id(b), []).append(i)
            for b in writes:
                if b.multi:
                    lastw.setdefault(id(b), []).append(i)
                else:
                    lastw[id(b)] = [i]
                    readers[id(b)] = []
            preds[i].discard(i)
        succ = [[] for _ in range(n)]
        indeg = [0] * n
        for i in range(n):
            indeg[i] = len(preds[i])
            for p in preds[i]:
                succ[p].append(i)
        import os
        import os
        LAT = float(os.environ.get('SCHED_LAT', '150'))
        PRIO = os.environ.get('SCHED_PRIO', '1') == '1'
        blev = [0.0] * n
        for i in range(n - 1, -1, -1):
            it = seg[i]
            d_ = it[5] + (2000.0 if it[0] == "dma" else 0.0)
            m_ = 0.0
            for s_ in succ[i]:
                if blev[s_] > m_:
                    m_ = blev[s_]
            blev[i] = d_ + m_
        fin = [0.0] * n
        efree = {e: 0.0 for e in ENGS}
        waitq = {e: [] for e in ENGS}
        readyq = {e: [] for e in ENGS}

        def push(i):
            e = seg[i][1]
            rt = 0.0
            for p in preds[i]:
                t = fin[p] + (LAT if seg[p][1] != e else 0.0)
                if t > rt:
                    rt = t
            heapq.heappush(waitq[e], (rt, i))

        def cand(e):
            w, r = waitq[e], readyq[e]
            while w and w[0][0] <= efree[e]:
                rt, i = heapq.heappop(w)
                heapq.heappush(r, ((-blev[i] if PRIO else i), i))
            if r:
                return (efree[e], r[0][0], r[0][1], True)
            if w:
                return (w[0][0], (-blev[w[0][1]] if PRIO else w[0][1]), w[0][1], False)
            return None

        for i in range(n):
            if indeg[i] == 0:
                push(i)
        order = []
        nreal = 0
        fill = self.cur_filler
        while nreal < n:
            best = None
            for e in ENGS:
                c_ = cand(e)
                if c_ is not None and (best is None or (c_[0], c_[1]) < (best[0][0], best[0][1])):
                    best = (c_, e)
            (st, _, i, from_ready), e = best
            if fill is not None and efree[fill[0]] > 0.0:
                fe, gap = fill[0], fill[5]
                while st - efree[fe] > gap:
                    order.append(-1)
                    efree[fe] += fill[4]
                    self.nfill += 1
            if from_ready:
                heapq.heappop(readyq[e])
            else:
                heapq.heappop(waitq[e])
            nreal += 1
            it = seg[i]
            if it[0] == "dma":
                issue = 500.0 if e == "sync" else 800.0
                efree[e] = st + issue
                fin[i] = st + issue + it[5]
            else:
                efree[e] = st + it[5]
                fin[i] = efree[e]
            order.append(i)
            for s_ in succ[i]:
                indeg[s_] -= 1
                if indeg[s_] == 0:
                    push(s_)
        return order

    def _flush(self, seg):
        if not seg:
            return
        order = self._schedule(seg) if self.sched else range(len(seg))
        for i in order:
            if i < 0:
                f_ = self.cur_filler
                self._op(f_[0], f_[1], f_[2], f_[3])
                continue
            it = seg[i]
            if it[0] == "op":
                self._op(it[1], it[2], it[3], it[4])
            else:
                out_buf, out_ap, in_buf, in_ap, owner, kw = it[2]
                self._dma(it[1], out_buf, out_ap, in_buf, in_ap, owner=owner, **kw)

    def emit(self):
        seg = []
        self.cur_filler = None
        for it in self.rec:
            if it[0] == "filler":
                self._flush(seg)
                seg = []
                self.cur_filler = it[1]
            elif it[0] == "barrier":
                self._flush(seg)
                seg = []
                self._barrier()
            elif it[0] == "finish":
                self._flush(seg)
                seg = []
                self._finish(it[1], it[2])
            else:
                seg.append(it)
        self._flush(seg)
        self._emit()

import math
import numpy as np

S = 4096
D = 1024
DFF = 4096
NBLK = S // 128
EPS = 1e-6
C1 = 6.28125
C2 = 2.0 * math.pi - 6.28125


class Arena:
    def __init__(self, P, nbytes):
        self.P = P
        self.base = P.sbuf("arena", [128, nbytes // 4], F32)
        self.top = 0
        self.cap = nbytes
        self.peak = 0

    def alloc(self, name, shape, dt):
        esz = 2 if dt == BF16 else 4
        nel = int(np.prod(shape[1:]))
        n = (nel * esz + 31) // 32 * 32
        off = self.top
        self.top += n
        assert self.top <= self.cap, ("arena overflow", name, self.top, self.cap)
        self.peak = max(self.peak, self.top)
        ap = self.base.t[0:shape[0], off // 4:(off + n) // 4]
        if dt != F32:
            ap = ap.bitcast(dt)
        ap = ap[:, 0:nel]
        if len(shape) == 3:
            ap = ap.rearrange("p (a b) -> p a b", a=shape[1])
        elif len(shape) == 4:
            ap = ap.rearrange("p (a b c) -> p a b c", a=shape[1], b=shape[2])
        return Buf(ap, name)

    def mark(self):
        return self.top

    def release(self, m):
        print("arena phase peak", self.peak, "release to", m)
        self.peak = m
        self.top = m


class Ctx:
    pass


def barrier(P):
    P.barrier()


def _fd(ap):
    n = 1
    for d_ in ap.shape[1:]:
        n *= int(d_)
    return n


def mm(C, ob, oap, lb, lap, rb, rap, start=True, stop=True):
    n = _fd(rap)
    import os
    est = (4.0 * n / 2.4 + 150.0) if lap.dtype == F32 else (n * float(os.environ.get('MM_NS', '0.45')) + 15.0)
    C.P.op("tensor", lambda h: h.matmul(oap, lhsT=lap, rhs=rap, start=start, stop=stop), [lb, rb], [ob], est=est)


def tr(C, ob, oap, ib, iap, idb, idap):
    C.P.op("tensor", lambda h: h.transpose(oap, iap, idap), [ib, idb], [ob], est=110.0)


def act(C, ob, oap, ib, iap, func, scale=1.0, bias=None, accum=None, rd=(), wr=()):
    kw = {}
    if bias is not None:
        kw["bias"] = bias
    if accum is not None:
        kw["accum_out"] = accum
    est = 230.0 + 0.83 * _fd(iap) + (90.0 if accum is not None else 0.0)
    C.P.op("scalar", lambda h: h.activation(out=oap, in_=iap, func=func, scale=scale, **kw),
           [ib] + list(rd), [ob] + list(wr), est=est)


def vec(C, name, kw, rd, wr, eng="vector"):
    o = kw.get("out", kw.get("ap"))
    n = _fd(o)
    if eng == "gpsimd":
        est = 250.0 + 2.3 * n
    elif name == "reciprocal":
        est = 120.0 + 6.6 * n
    elif name == "tensor_tensor_scan":
        est = 150.0 + 2.1 * n
    else:
        est = 110.0 + 1.0 * n
    C.P.op(eng, lambda h, name=name, kw=kw: getattr(h, name)(**kw), list(rd), list(wr), est=est)


def load_w(C, name, wd, kch, col0, ncols, eng="gpsimd", wb=None):
    if wb is None:
        wb = C.A.alloc(name, [128, kch, ncols], BF16)
    wb.multi = True
    if wd.name in C.WB:
        wd = C.WB[wd.name]
        eng = "sync"
    src = wd.t.rearrange("(c p) n -> p c n", p=128)
    step = 1 if ncols * 2 >= 4096 else 4
    for c in range(0, kch, step):
        c1 = min(kch, c + step)
        C.P.dma(eng, wb, wb[:, c:c1, :], wd, src[:, c:c1, col0:col0 + ncols])
    return wb


def bg_cast(C, names):
    owner = None
    for nm in names:
        wd = C.I[nm]
        rows, cols = wd.t.shape
        wb = C.P.dram(nm + "_bf", [rows, cols], BF16)
        wb.multi = True
        if owner is None:
            owner = wb
        nchunk = max(1, rows * cols // (1024 * 1024))
        rstep = rows // nchunk
        for r in range(0, rows, rstep):
            C.P.dma("gpsimd", wb, wb[r:r + rstep, :], wd, wd[r:r + rstep, :], owner=owner)
        C.WB[nm] = wb


def bank(C):
    b = C.pb[C.bi % getattr(C, 'nbank', 8)]
    C.bi += 1
    return b


def rms_rstd(C, ss, tmp, rstd, n):
    (sb, sap), (tb, tap), (rb, rap) = ss, tmp, rstd
    np_ = sap.shape[0]
    act(C, tb, tap, sb, sap, AF.Ln, scale=1.0 / n, bias=C.eps[0:np_, 0:1], rd=[C.eps])
    act(C, rb, rap, tb, tap, AF.Exp, scale=-0.5)


def norm_transpose(C, xin, xin_ap, gcol0, hT, col0, tmps):
    junk, ss, t1, rstd, xn = tmps
    act(C, junk, junk[:], xin, xin_ap, AF.Square, accum=ss[:, 0:1], wr=[ss])
    rms_rstd(C, (ss, ss[:, 0:1]), (t1, t1[:, 0:1]), (rstd, rstd[:, 0:1]), D)
    vec(C, "tensor_scalar", dict(out=xn[:], in0=xin_ap, scalar1=rstd[:, 0:1], scalar2=None, op0=ALU.mult),
        [xin, rstd], [xn])
    pb = bank(C)
    pbv = pb.t[:, 0:512].bitcast(BF16)
    for c in range(8):
        tr(C, pb, pbv[:, c * 128:(c + 1) * 128], xn, xn[:, c * 128:(c + 1) * 128], C.ident_b, C.ident_b[:])
    gap = C.gains[:, gcol0:gcol0 + 8].unsqueeze(2).to_broadcast([128, 8, 128])
    vec(C, "tensor_tensor", dict(out=hT[:, 0:8, col0:col0 + 128],
                                     in0=pbv.rearrange("p (c t) -> p c t", c=8), in1=gap, op=ALU.mult),
        [pb, C.gains], [hT])


def pe_filler(C, bankbuf, on=True, gap=40.0, est=40.0):
    import os
    if not on or os.environ.get("NOFILL"):
        C.P.set_filler(None)
        return
    dst = bankbuf[:, 0:64]
    dummy = Buf(dst, "pe_dummy")
    dummy.multi = True
    fn = lambda h: h.matmul(dst, lhsT=C.ident_b[:, 0:128], rhs=C.ident_b[:, 0:64], start=True, stop=True)
    C.P.set_filler(("tensor", fn, [C.ident_b], [dummy], float(os.environ.get("FILL_EST", est)), float(os.environ.get("FILL_GAP", gap))))


def phase0(C):
    P, A, I = C.P, C.A, C.I
    C.ident_f = A.alloc("ident_f", [128, 128], F32)
    P.dma("sync", C.ident_f, C.ident_f[:], I["ident"], I["ident"][:])
    C.ident_b = A.alloc("ident_b", [128, 128], BF16)
    vec(C, "tensor_copy", dict(out=C.ident_b[:], in_=C.ident_f[:]), [C.ident_f], [C.ident_b])
    C.tri_f = A.alloc("tri_f", [128, 128], F32)
    P.dma("sync", C.tri_f, C.tri_f[:], I["tri"], I["tri"][:])
    C.tri_b = A.alloc("tri_b", [128, 128], BF16)
    vec(C, "tensor_copy", dict(out=C.tri_b[:], in_=C.tri_f[:]), [C.tri_f], [C.tri_b])
    C.i4 = A.alloc("i4", [4, 4], F32)
    P.dma("sync", C.i4, C.i4[:], I["ident"], I["ident"][0:4, 0:4])
    C.ones4 = A.alloc("ones4", [4, 512], F32)
    vec(C, "memset", dict(ap=C.ones4[:], constant=1.0), [], [C.ones4])
    C.eps = A.alloc("eps", [128, 1], F32)
    vec(C, "memset", dict(ap=C.eps[:], constant=EPS), [], [C.eps])
    C.one = A.alloc("one", [128, 1], F32)
    vec(C, "memset", dict(ap=C.one[:], constant=1.0), [], [C.one])
    C.gains = A.alloc("gains", [128, 40], F32)
    P.dma("sync", C.gains, C.gains[:], I["gains"], I["gains"][:])
    C.smalls = A.alloc("smalls", [128, 8], F32)
    P.dma("sync", C.smalls, C.smalls[:], I["smalls"], I["smalls"][:])
    C.mem_kT = A.alloc("mem_kT", [128, 2, 256], BF16)
    C.vpad = A.alloc("vpad", [128, 2, 4, 128], BF16)
    C.onespad = A.alloc("onespad", [128, 2, 128], BF16)
    vec(C, "memset", dict(ap=C.vpad[:], constant=0.0), [], [C.vpad])
    vec(C, "memset", dict(ap=C.onespad[:], constant=0.0), [], [C.onespad])
    vec(C, "memset", dict(ap=C.onespad[:, 0, 0:64], constant=1.0), [], [C.onespad])
    vec(C, "memset", dict(ap=C.onespad[:, 1, 64:128], constant=1.0), [], [C.onespad])

    m0 = A.mark()
    wkv = load_w(C, "wkv", I["w_mem_kv"], 8, 0, 512)
    memT = A.alloc("memT", [128, 8, 256], BF16)
    junk = A.alloc("junk", [128, 1024], BF16)
    ss = A.alloc("ss", [128, 1], F32)
    t1 = A.alloc("t1", [128, 1], F32)
    rstd = A.alloc("rstd", [128, 1], F32)
    xn = A.alloc("xn", [128, 1024], BF16)
    for mb in range(2):
        mx = A.alloc("memx%d" % mb, [128, 1024], F32)
        P.dma("sync", mx, mx[:], I["mem"], I["mem"][mb * 128:(mb + 1) * 128, :])
        norm_transpose(C, mx, mx[:], 32, memT, mb * 128, (junk, ss, t1, rstd, xn))
    for pair in range(2):
        pb = bank(C)
        for c in range(8):
            mm(C, pb, pb[:, 0:256], wkv, wkv[:, c, pair * 128:(pair + 1) * 128], memT, memT[:, c, :],
               start=(c == 0), stop=(c == 7))
        act(C, C.mem_kT, C.mem_kT[:, pair, :], pb, pb[:, 0:256], AF.Copy)
    for mb in range(2):
        pb = bank(C)
        for c in range(8):
            mm(C, pb, pb[:, 0:256], memT, memT[:, c, mb * 128:(mb + 1) * 128], wkv, wkv[:, c, 256:512],
               start=(c == 0), stop=(c == 7))
        for hh in range(4):
            i = hh % 2
            act(C, C.vpad, C.vpad[:, mb, hh, 64 * i:64 * i + 64], pb, pb[:, hh * 64:(hh + 1) * 64], AF.Copy)

    CH = 1024
    posi = A.alloc("posi", [128, CH], I32)
    posf = A.alloc("posf", [128, CH], F32)
    ang = A.alloc("ang", [128, CH], F32)
    a2 = A.alloc("a2", [128, CH], F32)
    kq = A.alloc("kq", [128, CH], I32)
    kf = A.alloc("kf", [128, CH], F32)
    rr = A.alloc("rr", [128, CH], F32)
    so = [A.alloc("so%d" % i, [128, CH], F32) for i in range(2)]
    PI = math.pi
    k = 0
    for ch in range(S // CH):
        cs = slice(ch * CH, (ch + 1) * CH)
        P.dma("sync", posi, posi[:], I["pos"], I["pos"][0:1, cs].to_broadcast([128, CH]))
        vec(C, "tensor_copy", dict(out=posf[:], in_=posi[:]), [posi], [posf])
        vec(C, "tensor_scalar", dict(out=ang[:], in0=posf[:], scalar1=C.smalls[:, 5:6], scalar2=None,
                                         op0=ALU.mult), [posf, C.smalls], [ang])
        for which in range(2):
            src = ang
            if which == 1:
                vec(C, "tensor_scalar", dict(out=a2[:], in0=ang[:], scalar1=PI / 2, scalar2=None, op0=ALU.add),
                    [ang], [a2])
                src = a2
            vec(C, "tensor_scalar", dict(out=kq[:], in0=src[:], scalar1=1.0 / (2 * PI), scalar2=None,
                                                      op0=ALU.mult), [src], [kq])
            vec(C, "tensor_copy", dict(out=kf[:], in_=kq[:]), [kq], [kf])
            vec(C, "scalar_tensor_tensor", dict(out=rr[:], in0=kf[:], scalar=-C1, in1=src[:],
                                                             op0=ALU.mult, op1=ALU.add), [kf, src], [rr])
            vec(C, "scalar_tensor_tensor", dict(out=rr[:], in0=kf[:], scalar=-C2, in1=rr[:],
                                                    op0=ALU.mult, op1=ALU.add), [kf, rr], [rr])
            vec(C, "tensor_scalar", dict(out=rr[:], in0=rr[:], scalar1=-PI, scalar2=PI, op0=ALU.max,
                                             op1=ALU.min), [rr], [rr])
            o = so[k % 2]
            k += 1
            act(C, o, o[:], rr, rr[:], AF.Sin)
            if which == 0:
                vec(C, "tensor_scalar", dict(out=o[:], in0=o[:], scalar1=C.smalls[:, 6:7], scalar2=None,
                                                      op0=ALU.mult), [o, C.smalls], [o])
                P.dma("sync", C.ROPES, C.ROPES[:, cs], o, o[:])
            else:
                P.dma("sync", C.ROPEC, C.ROPEC[:, cs], o, o[:])
    barrier(P)
    A.release(m0)


QK0, V0, O0, IG0, FG0, QM0 = 0, 768, 1536, 2304, 2308, 2312


def l0mix(C, T=256):
    P, A, I = C.P, C.A, C.I
    NB = T // 128
    NT = S // T
    m0 = A.mark()
    C.nbank = 5
    pe_filler(C, C.pb[5])
    w_in = load_w(C, "w_in0", I["w_in0"], 8, 0, 2568)
    w_out = load_w(C, "w_out0", I["w_out0"], 8, 0, 1024)
    if C.bgcast:
        bg_cast(C, ["w_ff1_0", "w_ff2_0"])
        bg_cast(C, ["w_in1", "w_krB", "w_uq_nope", "w_uq_rA", "w_uq_rB", "w_uk", "w_uv"])
        bg_cast(C, ["w_ff1_1", "w_ff2_1", "w_out1"])
    convT = A.alloc("convT", [96, 32], F32)
    P.dma("sync", convT, convT[:], I["convT"], I["convT"][:])
    bif = A.alloc("bif", [4, 2], F32)
    P.dma("sync", bif, bif[:], I["bif"], I["bif"][:])
    whn = A.alloc("whn", [128, 768], F32)
    P.dma("sync", whn, whn[:], I["whn"], I["whn"][0:1, :].to_broadcast([128, 768]))
    Cn = A.alloc("Cn", [96, 4, 193], F32)
    vec(C, "memset", dict(ap=Cn[:], constant=0.0), [], [Cn])
    halo = A.alloc("halo", [96, 8, 3], F32)
    vec(C, "memset", dict(ap=halo[:], constant=0.0), [], [halo])
    Blast = A.alloc("Blast", [4, 1], F32)
    Glast = A.alloc("Glast", [4, 1], F32)
    vec(C, "memset", dict(ap=Blast[:], constant=0.0), [], [Blast])
    vec(C, "memset", dict(ap=Glast[:], constant=0.0), [], [Glast])

    S1 = []
    for i in range(2):
        d = Ctx()
        d.hT = A.alloc("hT%d" % i, [128, 8, T], BF16)
        d.qkT = A.alloc("qkT%d" % i, [96, 8, T], BF16)
        d.vaug = A.alloc("vaug%d" % i, [128, NB, 4, 193], BF16)
        vec(C, "memset", dict(ap=d.vaug[:], constant=1.0), [], [d.vaug])
        d.TM = A.alloc("TM%d" % i, [128, NB * 8], F32)
        d.DEC = A.alloc("DEC%d" % i, [128, NB * 4], F32)
        S1.append(d)
    qmT3 = [A.alloc("qmT%d" % i, [128, 2, T], BF16) for i in range(3)]
    xin = [A.alloc("xin%d" % i, [128, 1024], F32) for i in range(2)]
    junk = A.alloc("junk", [128, 1024], BF16)
    xn = [A.alloc("xn%d" % i, [128, 1024], BF16) for i in range(2)]
    ss = [A.alloc("ss%d" % i, [128, 1], F32) for i in range(2)]
    t1 = [A.alloc("t1%d" % i, [128, 1], F32) for i in range(2)]
    rstd = [A.alloc("rstd%d" % i, [128, 1], F32) for i in range(2)]
    pre = [A.alloc("pre%d" % i, [96, T + 3], F32) for i in range(2)]
    cacc = [A.alloc("cacc%d" % i, [96, T], F32) for i in range(2)]
    ce = [A.alloc("ce%d" % i, [96, T], F32) for i in range(2)]
    rows = {}
    for nm in ("ig", "z", "a", "e", "mn", "lf", "Bc", "gp", "Gc", "ur", "pr"):
        rows[nm] = A.alloc("row_" + nm, [4, T], F32)
    murho = A.alloc("murho", [4, 2, NB], F32)
    dd = A.alloc("dd", [4, NB], F32)
    Rdec = A.alloc("Rdec", [4, NB, 4], F32)
    og = [A.alloc("og%d" % i, [128, 768], F32) for i in range(2)]
    wgo = [A.alloc("wgo%d" % i, [128, 768], BF16) for i in range(2)]
    sT = [A.alloc("sT%d" % i, [128, 4, 128], BF16) for i in range(2)]
    uk = [A.alloc("uk%d" % i, [128, 4, 96], BF16) for i in range(2)]
    Dbf = [A.alloc("Dbf%d" % i, [96, 2, 193], BF16) for i in range(4)]
    yml = [A.alloc("yml%d" % i, [128, 768], BF16) for i in range(2)]
    yT = [A.alloc("yT%d" % i, [128, 8, T], BF16) for i in range(2)]
    eT = [A.alloc("eT%d" % i, [128, T], BF16) for i in range(4)]
    rec = [A.alloc("rec%d" % i, [128, T], F32) for i in range(2)]
    xres = [A.alloc("xres%d" % i, [128, 1024], F32) for i in range(2)]
    xo = [A.alloc("xo%d" % i, [128, 1024], F32) for i in range(2)]
    junk2 = A.alloc("junk2", [128, 192], BF16)
    den = [A.alloc("den%d" % i, [128, 2], F32) for i in range(4)]
    rcp = [A.alloc("rcp%d" % i, [128, 2], F32) for i in range(4)]
    ss2 = [A.alloc("ss2%d" % i, [128, 2], F32) for i in range(4)]
    t2 = [A.alloc("t2%d" % i, [128, 2], F32) for i in range(4)]
    rs2 = [A.alloc("rs2%d" % i, [128, 2], F32) for i in range(4)]
    cnt = {"x": 0, "g": 0, "e": 0, "p": 0}

    def qk_group(s, g):
        k = g % 2
        pb = bank(C)
        for c in range(8):
            mm(C, pb, pb[0:96, 0:T], w_in, w_in[:, c, QK0 + g * 96:QK0 + (g + 1) * 96], s.hT, s.hT[:, c, :],
               start=(c == 0), stop=(c == 7))
        pr, ca, e = pre[k], cacc[k], ce[k]
        vec(C, "tensor_copy", dict(out=pr[:, 0:3], in_=halo[:, g, :]), [halo], [pr])
        act(C, pr, pr[:, 3:3 + T], pb, pb[0:96, 0:T], AF.Copy)
        yield
        vec(C, "tensor_copy", dict(out=halo[:, g, :], in_=pr[:, T:T + 3]), [pr], [halo])
        vec(C, "tensor_scalar", dict(out=ca[:], in0=pr[:, 3:3 + T], scalar1=convT[:, g * 4 + 3:g * 4 + 4], scalar2=None,
                                     op0=ALU.mult), [pr, convT], [ca])
        yield
        for j in (2, 1, 0):
            vec(C, "scalar_tensor_tensor", dict(
                out=ca[:], in0=pr[:, j:j + T], scalar=convT[:, g * 4 + j:g * 4 + j + 1], in1=ca[:],
                op0=ALU.mult, op1=ALU.add), [pr, convT, ca], [ca])
            yield
        act(C, e, e[:], ca, ca[:], AF.Exp, scale=-1.0)
        yield
        act(C, e, e[:], e, e[:], AF.Ln, bias=C.one[0:96, 0:1], rd=[C.one])
        yield
        act(C, e, e[:], e, e[:], AF.Exp, scale=-1.0)
        yield
        cst = 1.0 if g < 4 else 96.0 ** -0.5
        vec(C, "scalar_tensor_tensor", dict(
            out=s.qkT[:, g, :], in0=ca[:], scalar=cst, in1=e[:], op0=ALU.mult, op1=ALU.mult), [ca, e], [s.qkT])

    def stage1(t):
        s = S1[t % 2]
        for b in range(NB):
            k = cnt["x"] % 2
            cnt["x"] += 1
            r0 = t * T + b * 128
            P.dma("sync", xin[k], xin[k][:], I["x"], I["x"][r0:r0 + 128, :])
            norm_transpose(C, xin[k], xin[k][:], 0, s.hT, b * 128, (junk, ss[k], t1[k], rstd[k], xn[k]))
            yield
        for g in range(0, 8, 2):
            ga, gb = qk_group(s, g), qk_group(s, g + 1)
            for _ in ga:
                next(gb, None)
            for _ in gb:
                pass
            yield
        R = rows
        pbg = bank(C)
        for c in range(8):
            mm(C, pbg, pbg[0:4, 0:T], w_in, w_in[:, c, IG0:IG0 + 4], s.hT, s.hT[:, c, :], start=(c == 0), stop=(c == 7))
        act(C, R["ig"], R["ig"][:], pbg, pbg[0:4, 0:T], AF.Identity, bias=bif[:, 0:1], rd=[bif])
        pbf = bank(C)
        for c in range(8):
            mm(C, pbf, pbf[0:4, 0:T], w_in, w_in[:, c, FG0:FG0 + 4], s.hT, s.hT[:, c, :], start=(c == 0), stop=(c == 7))
        act(C, R["z"], R["z"][:], pbf, pbf[0:4, 0:T], AF.Identity, bias=bif[:, 1:2], rd=[bif])
        act(C, R["a"], R["a"][:], R["z"], R["z"][:], AF.Abs)
        act(C, R["e"], R["e"][:], R["a"], R["a"][:], AF.Exp, scale=-1.0)
        act(C, R["e"], R["e"][:], R["e"], R["e"][:], AF.Ln, bias=C.one[0:4, 0:1], rd=[C.one])
        vec(C, "tensor_scalar", dict(out=R["mn"][:], in0=R["z"][:], scalar1=0.0, scalar2=None, op0=ALU.min),
            [R["z"]], [R["mn"]])
        vec(C, "tensor_tensor", dict(out=R["lf"][:], in0=R["mn"][:], in1=R["e"][:], op=ALU.subtract),
            [R["mn"], R["e"]], [R["lf"]])
        vec(C, "tensor_tensor_scan", dict(out=R["Bc"][:], data0=C.ones4[:, 0:T], data1=R["lf"][:],
                                          initial=Blast[:, 0:1], op0=ALU.mult, op1=ALU.add),
            [C.ones4, R["lf"], Blast], [R["Bc"]])
        vec(C, "tensor_tensor", dict(out=R["gp"][:], in0=R["ig"][:], in1=R["Bc"][:], op=ALU.subtract),
            [R["ig"], R["Bc"]], [R["gp"]])
        vec(C, "tensor_tensor_scan", dict(out=R["Gc"][:], data0=R["gp"][:], data1=R["gp"][:],
                                          initial=Glast[:, 0:1], op0=ALU.max, op1=ALU.max),
            [R["gp"], Glast], [R["Gc"]])
        yield
        Gv = R["Gc"][:].rearrange("p (b t) -> p b t", b=NB)
        vec(C, "tensor_copy", dict(out=murho[:, 1, :], in_=Gv[:, :, 127]), [R["Gc"]], [murho])
        vec(C, "tensor_copy", dict(out=murho[:, 0, 0:1], in_=Glast[:, 0:1]), [Glast], [murho])
        if NB > 1:
            vec(C, "tensor_copy", dict(out=murho[:, 0, 1:NB], in_=murho[:, 1, 0:NB - 1]), [murho], [murho])
        vec(C, "tensor_copy", dict(out=Blast[:, 0:1], in_=R["Bc"][:, T - 1:T]), [R["Bc"]], [Blast])
        vec(C, "tensor_copy", dict(out=Glast[:, 0:1], in_=R["Gc"][:, T - 1:T]), [R["Gc"]], [Glast])
        rho_bc = murho[:, 1, :].unsqueeze(2).to_broadcast([4, NB, 128])
        vec(C, "tensor_tensor", dict(out=R["ur"][:].rearrange("p (b t) -> p b t", b=NB),
                                     in0=R["gp"][:].rearrange("p (b t) -> p b t", b=NB), in1=rho_bc,
                                     op=ALU.subtract), [R["gp"], murho], [R["ur"]])
        act(C, R["ur"], R["ur"][:], R["ur"], R["ur"][:], AF.Exp)
        vec(C, "tensor_tensor", dict(out=R["pr"][:].rearrange("p (b t) -> p b t", b=NB),
                                     in0=R["Bc"][:].rearrange("p (b t) -> p b t", b=NB), in1=rho_bc,
                                     op=ALU.add), [R["Bc"], murho], [R["pr"]])
        act(C, R["pr"], R["pr"][:], R["pr"], R["pr"][:], AF.Exp, scale=-1.0)
        vec(C, "tensor_tensor", dict(out=dd[:], in0=murho[:, 0, :], in1=murho[:, 1, :], op=ALU.subtract),
            [murho], [dd])
        act(C, dd, dd[:], dd, dd[:], AF.Exp)
        yield
        pbt = bank(C)
        for b in range(NB):
            mm(C, pbt, pbt[:, b * 8:b * 8 + 4], R["ur"], R["ur"][:, b * 128:(b + 1) * 128], C.i4, C.i4[:])
            mm(C, pbt, pbt[:, b * 8 + 4:b * 8 + 8], R["pr"], R["pr"][:, b * 128:(b + 1) * 128], C.i4, C.i4[:])
        vec(C, "tensor_copy", dict(out=s.TM[:], in_=pbt[:, 0:NB * 8]), [pbt], [s.TM])
        vec(C, "tensor_tensor", dict(out=Rdec[:], in0=dd[:].unsqueeze(2).to_broadcast([4, NB, 4]),
                                     in1=C.i4[:].unsqueeze(1).to_broadcast([4, NB, 4]), op=ALU.mult),
            [dd, C.i4], [Rdec])
        pbd = bank(C)
        mm(C, pbd, pbd[:, 0:NB * 4], C.ones4, C.ones4[:, 0:128], Rdec, Rdec[:].rearrange("p b h -> p (b h)"))
        vec(C, "tensor_copy", dict(out=s.DEC[:], in_=pbd[:, 0:NB * 4]), [pbd], [s.DEC])
        yield
        for b in range(NB):
            for half in range(2):
                pb = bank(C)
                for c in range(8):
                    mm(C, pb, pb[:, 0:384], s.hT, s.hT[:, c, b * 128:(b + 1) * 128], w_in,
                       w_in[:, c, V0 + half * 384:V0 + (half + 1) * 384], start=(c == 0), stop=(c == 7))
                act(C, s.vaug, s.vaug[:, b, 2 * half:2 * half + 2, 0:192], pb,
                    pb[:, 0:384].rearrange("p (a d) -> p a d", a=2), AF.Copy)
            yield
        for pair in range(2):
            pb = bank(C)
            for c in range(8):
                mm(C, pb, pb[:, 0:T], w_in, w_in[:, c, QM0 + pair * 128:QM0 + (pair + 1) * 128], s.hT, s.hT[:, c, :],
                   start=(c == 0), stop=(c == 7))
            act(C, qmT3[t % 3], qmT3[t % 3][:, pair, :], pb, pb[:, 0:T], AF.Copy)

    def stage2(t):
        s = S1[t % 2]
        y = yT[t % 2]

        def F(b):
            blk = t * NB + b
            k = blk % 2
            cs = slice(b * 128, (b + 1) * 128)
            for half in range(2):
                pb = bank(C)
                for c in range(8):
                    mm(C, pb, pb[:, 0:384], s.hT, s.hT[:, c, cs], w_in,
                       w_in[:, c, O0 + half * 384:O0 + (half + 1) * 384], start=(c == 0), stop=(c == 7))
                act(C, og[k], og[k][:, half * 384:(half + 1) * 384], pb, pb[:, 0:384], AF.Exp, scale=-1.0)
            pst = bank(C)
            for hh in range(4):
                mm(C, pst, pst[:, hh * 128:(hh + 1) * 128], s.qkT, s.qkT[:, 4 + hh, cs], s.qkT, s.qkT[:, hh, cs])
            ptk = bank(C)
            ptkv = ptk.t[:, 0:512].bitcast(BF16)
            for hh in range(4):
                tr(C, ptk, ptkv[:, hh * 96:(hh + 1) * 96], s.qkT, s.qkT[:, 4 + hh, cs], C.ident_b, C.ident_b[0:96, 0:96])
            for hh in range(4):
                vec(C, "scalar_tensor_tensor", dict(
                    out=sT[k][:, hh, :], in0=pst[:, hh * 128:(hh + 1) * 128], scalar=s.TM[:, b * 8 + hh:b * 8 + hh + 1],
                    in1=C.tri_f[:], op0=ALU.mult, op1=ALU.mult), [pst, s.TM, C.tri_f], [sT[k]])
                act(C, uk[k], uk[k][:, hh, :], ptk, ptkv[:, hh * 96:(hh + 1) * 96], AF.Copy,
                    scale=s.TM[:, b * 8 + hh:b * 8 + hh + 1], rd=[s.TM])
            act(C, og[k], og[k][:], og[k], og[k][:], AF.Ln, bias=C.one[:, 0:1], rd=[C.one])
            act(C, og[k], og[k][:], og[k], og[k][:], AF.Exp, scale=-1.0)
            vec(C, "tensor_tensor", dict(out=wgo[k][:], in0=og[k][:], in1=whn[:], op=ALU.mult),
                [og[k], whn], [wgo[k]], eng="gpsimd")
            for hp in range(2):
                kk = 2 * k + hp
                pu = bank(C)
                for q in range(2):
                    hh = 2 * hp + q
                    mm(C, pu, pu[0:96, q * 256:q * 256 + 193], uk[k], uk[k][:, hh, :], s.vaug, s.vaug[:, b, hh, :])
                for q in range(2):
                    hh = 2 * hp + q
                    dcol = s.DEC[0:96, b * 4 + hh:b * 4 + hh + 1]
                    vec(C, "tensor_scalar", dict(out=Dbf[kk][:, q, :], in0=Cn[:, hh, :], scalar1=dcol, scalar2=None,
                                                 op0=ALU.mult), [Cn, s.DEC], [Dbf[kk]])
                    vec(C, "scalar_tensor_tensor", dict(
                        out=Cn[:, hh, :], in0=Cn[:, hh, :], scalar=dcol, in1=pu[0:96, q * 256:q * 256 + 193],
                        op0=ALU.mult, op1=ALU.add), [Cn, s.DEC, pu], [Cn])

        def G(b):
            blk = t * NB + b
            k = blk % 2
            cs = slice(b * 128, (b + 1) * 128)
            paccs = []
            for hp in range(2):
                kk = 2 * k + hp
                pacc = C.pb[6 + hp]
                paccs.append(pacc)
                for q in range(2):
                    hh = 2 * hp + q
                    mm(C, pacc, pacc[:, q * 256:q * 256 + 193], sT[k], sT[k][:, hh, :], s.vaug, s.vaug[:, b, hh, :],
                       start=True, stop=False)
                    mm(C, pacc, pacc[:, q * 256:q * 256 + 193], s.qkT, s.qkT[:, hh, cs], Dbf[kk], Dbf[kk][:, q, :],
                       start=False, stop=True)
            yield
            pvs = [p_[:, 0:512].rearrange("p (q d) -> p q d", q=2) for p_ in paccs]
            for hp in range(2):
                act(C, den[2 * k + hp], den[2 * k + hp][:], paccs[hp], pvs[hp][:, :, 192], AF.Abs)
            for hp in range(2):
                vec(C, "tensor_tensor", dict(out=den[2 * k + hp][:], in0=den[2 * k + hp][:],
                                             in1=s.TM[:, b * 8 + 4 + 2 * hp:b * 8 + 6 + 2 * hp], op=ALU.max),
                    [den[2 * k + hp], s.TM], [den[2 * k + hp]])
            yield
            for hp in range(2):
                vec(C, "reciprocal", dict(out=rcp[2 * k + hp][:], in_=den[2 * k + hp][:]), [den[2 * k + hp]], [rcp[2 * k + hp]])
            yield
            for hp in range(2):
                for q in range(2):
                    act(C, junk2, junk2[:], paccs[hp], paccs[hp][:, q * 256:q * 256 + 192], AF.Square,
                        scale=rcp[2 * k + hp][:, q:q + 1], accum=ss2[2 * k + hp][:, q:q + 1], rd=[rcp[2 * k + hp]], wr=[ss2[2 * k + hp]])
            yield
            for hp in range(2):
                act(C, t2[2 * k + hp], t2[2 * k + hp][:], ss2[2 * k + hp], ss2[2 * k + hp][:], AF.Ln, scale=1.0 / 192, bias=C.eps[:, 0:1], rd=[C.eps])
            for hp in range(2):
                act(C, rs2[2 * k + hp], rs2[2 * k + hp][:], t2[2 * k + hp], t2[2 * k + hp][:], AF.Exp, scale=-0.5)
            yield
            for hp in range(2):
                vec(C, "tensor_tensor", dict(out=rs2[2 * k + hp][:], in0=rs2[2 * k + hp][:], in1=rcp[2 * k + hp][:], op=ALU.mult),
                    [rs2[2 * k + hp], rcp[2 * k + hp]], [rs2[2 * k + hp]])
            yield
            for q in range(2):
                for hp in range(2):
                    hh = 2 * hp + q
                    vec(C, "scalar_tensor_tensor", dict(
                        out=yml[k][:, hh * 192:(hh + 1) * 192], in0=paccs[hp][:, q * 256:q * 256 + 192],
                        scalar=rs2[2 * k + hp][:, q:q + 1], in1=wgo[k][:, hh * 192:(hh + 1) * 192], op0=ALU.mult, op1=ALU.mult),
                        [paccs[hp], rs2[2 * k + hp], wgo[k]], [yml[k]])

        def H(b):
            blk = t * NB + b
            k = blk % 2
            cs = slice(b * 128, (b + 1) * 128)
            pty = bank(C)
            ptyv = pty.t[:, 0:512].bitcast(BF16)
            for c in range(6):
                tr(C, pty, ptyv[:, c * 128:(c + 1) * 128], yml[k], yml[k][:, c * 128:(c + 1) * 128], C.ident_b, C.ident_b[:])
            act(C, y, y[:, 0:6, cs], pty, ptyv[:, 0:768].rearrange("p (c t) -> p c t", c=6), AF.Copy)

        seq = []
        for b in range(NB):
            seq.append((F, b))
        for b in range(NB):
            seq.append((G, b))
            if b >= 1:
                seq.append((H, b - 1))
        seq.append((H, NB - 1))
        for fn, b in seq:
            r_ = fn(b)
            if r_ is not None:
                for _ in r_:
                    yield
            yield

    def stage3(t):
        y = yT[t % 2]

        def O(b):
            blk = t * NB + b
            k = blk % 2
            cs = slice(b * 128, (b + 1) * 128)
            r0 = blk * 128
            P.dma("sync", xres[k], xres[k][:], I["x"], I["x"][r0:r0 + 128, :])
            for half in range(2):
                pb = bank(C)
                for c in range(8):
                    mm(C, pb, pb[:, 0:512], y, y[:, c, cs], w_out, w_out[:, c, half * 512:(half + 1) * 512],
                       start=(c == 0), stop=(c == 7))
                vec(C, "tensor_tensor", dict(out=xo[k][:, half * 512:(half + 1) * 512], in0=pb[:, 0:512],
                                             in1=xres[k][:, half * 512:(half + 1) * 512], op=ALU.add),
                    [pb, xres[k]], [xo[k]])
            P.dma("gpsimd", C.X1, C.X1[r0:r0 + 128, :], xo[k], xo[k][:])

        mem_attn(C, qmT3[t % 3], y, T, eT, rec, cnt, pairs=(0,))
        yield
        mem_attn(C, qmT3[t % 3], y, T, eT, rec, cnt, pairs=(1,))
        yield
        for b in range(NB):
            O(b)
            yield

    def merge(gens):
        alive = list(gens)
        while alive:
            for g_, w_ in list(alive):
                try:
                    for _ in range(w_):
                        next(g_)
                except StopIteration:
                    alive.remove((g_, w_))

    for t in range(NT + 2):
        gens = []
        if t < NT:
            gens.append((stage1(t), 1))
        if 1 <= t <= NT:
            gens.append((stage2(t - 1), 2))
        if t >= 2:
            gens.append((stage3(t - 2), 1))
        merge(gens)
    C.nbank = 8
    barrier(P)
    pe_filler(C, None, on=False)
    A.release(m0)


def mem_attn(C, qmT, y, T, eT, rec, cnt, pairs=(0, 1)):
    for pair in pairs:
        psc0 = bank(C)
        pn = bank(C)
        pd = bank(C)
        n = 0
        for i in range(2):
            hh = 2 * pair + i
            for mc in range(2):
                psc = psc0 if n == 0 else bank(C)
                mm(C, psc, psc[:, 0:T], C.mem_kT, C.mem_kT[64 * i:64 * i + 64, pair, mc * 128:(mc + 1) * 128],
                   qmT, qmT[64 * i:64 * i + 64, pair, :])
                e = eT[cnt["e"] % 4]
                cnt["e"] += 1
                act(C, e, e[:], psc, psc[:, 0:T], AF.Exp, scale=0.125)
                mm(C, pn, pn[:, 0:T], C.vpad, C.vpad[:, mc, hh, :], e, e[:], start=(n == 0), stop=(n == 3))
                mm(C, pd, pd[:, 0:T], C.onespad, C.onespad[:, i, :], e, e[:], start=(n == 0), stop=(n == 3))
                n += 1
        r = rec[pair]
        act(C, r, r[:], pd, pd[:, 0:T], AF.Ln)
        act(C, r, r[:], r, r[:], AF.Exp, scale=-1.0)
        vec(C, "tensor_tensor", dict(out=y[:, 6 + pair, :], in0=pn[:, 0:T], in1=r[:],
                                                                op=ALU.mult), [pn, r], [y])

def ffn(C, Xin, Xout, gcol0, w1d, w2d, final=False, T=256, pre=None):
    P, A, I = C.P, C.A, C.I
    NB = T // 128
    NT = S // T
    m0 = A.mark()
    C.nbank = 7
    pe_filler(C, C.pb[7])
    W1 = pre[0] if pre is not None else load_w(C, "W1", w1d, 8, 0, DFF)
    W2 = load_w(C, "W2", w2d, 32, 0, D)
    hT = A.alloc("hT", [128, 8, T], BF16)
    uT = A.alloc("uT", [128, 32, T], BF16)
    xin = [A.alloc("xin%d" % i, [128, 1024], F32) for i in range(2)]
    xres = [A.alloc("xres%d" % i, [128, 1024], F32) for i in range(2)]
    xo = [A.alloc("xo%d" % i, [128, 1024], F32) for i in range(2)]
    rr = [A.alloc("rr%d" % i, [128, T], F32) for i in range(2)]
    junk = A.alloc("junk", [128, 1024], BF16)
    xn = A.alloc("xn", [128, 1024], BF16)
    ss = [A.alloc("ss%d" % i, [128, 1], F32) for i in range(2)]
    t1 = [A.alloc("t1%d" % i, [128, 1], F32) for i in range(2)]
    rstd = [A.alloc("rstd%d" % i, [128, 1], F32) for i in range(2)]
    if final:
        gfin = A.alloc("gfin", [128, 1024], F32)
        P.dma("sync", gfin, gfin[:], I["gfinal"], I["gfinal"][0:1, :].to_broadcast([128, 1024]))
    cnt = {"x": 0, "r": 0}

    def A1(t):
        for b in range(NB):
            k = cnt["x"] % 2
            cnt["x"] += 1
            r0 = t * T + b * 128
            P.dma("sync", xin[k], xin[k][:], Xin, Xin[r0:r0 + 128, :])
            norm_transpose(C, xin[k], xin[k][:], gcol0, hT, b * 128, (junk, ss[k], t1[k], rstd[k], xn))

    def A2(t):
        for f in range(32):
            pb = bank(C)
            for c in range(8):
                mm(C, pb, pb[:, 0:T], W1, W1[:, c, f * 128:(f + 1) * 128], hT, hT[:, c, :], start=(c == 0), stop=(c == 7))
            k = cnt["r"] % 2
            cnt["r"] += 1
            act(C, rr[k], rr[k][:], pb, pb[:, 0:T], AF.Relu)
            vec(C, "tensor_tensor", dict(out=uT[:, f, :], in0=rr[k][:], in1=rr[k][:], op=ALU.mult), [rr[k]], [uT],
                eng="gpsimd")

    def B(t):
        for b in range(NB):
            blk = t * NB + b
            k = blk % 2
            r0 = blk * 128
            cs = slice(b * 128, (b + 1) * 128)
            P.dma("sync", xres[k], xres[k][:], Xin, Xin[r0:r0 + 128, :])
            for half in range(2):
                pb = bank(C)
                for f in range(32):
                    mm(C, pb, pb[:, 0:512], uT, uT[:, f, cs], W2, W2[:, f, half * 512:(half + 1) * 512],
                       start=(f == 0), stop=(f == 31))
                vec(C, "tensor_tensor", dict(out=xo[k][:, half * 512:(half + 1) * 512], in0=pb[:, 0:512],
                                             in1=xres[k][:, half * 512:(half + 1) * 512], op=ALU.add),
                    [pb, xres[k]], [xo[k]])
            if final:
                act(C, junk, junk[:], xo[k], xo[k][:], AF.Square, accum=ss[k][:, 0:1], wr=[ss[k]])
                rms_rstd(C, (ss[k], ss[k][:, 0:1]), (t1[k], t1[k][:, 0:1]), (rstd[k], rstd[k][:, 0:1]), D)
                vec(C, "scalar_tensor_tensor", dict(out=xo[k][:], in0=xo[k][:], scalar=rstd[k][:, 0:1], in1=gfin[:],
                                                    op0=ALU.mult, op1=ALU.mult), [xo[k], rstd[k], gfin], [xo[k]])
            P.dma("gpsimd", Xout, Xout[r0:r0 + 128, :], xo[k], xo[k][:])

    A1(0)
    A2(0)
    for t in range(NT):
        if t + 1 < NT:
            A1(t + 1)
        B(t)
        if t + 1 < NT:
            A2(t + 1)
    C.nbank = 8
    barrier(P)
    pe_filler(C, None, on=False)
    A.release(m0)


CQ0, CKV0, KR0, QM1 = 0, 384, 640, 672


def feat_rmsnorm(C, pbs, nch, gcol, outT, T, sq, rbc, ones_f):
    pst = bank(C)
    for j in range(nch):
        q = sq[j % 2]
        act(C, q, q[:], pbs[j], pbs[j][:, 0:T], AF.Square)
        mm(C, pst, pst[:, 0:T], ones_f, ones_f[:], q, q[:], start=(j == 0), stop=(j == nch - 1))
    act(C, rbc, rbc[:], pst, pst[:, 0:T], AF.Ln, scale=1.0 / (nch * 128), bias=C.eps[:, 0:1], rd=[C.eps])
    act(C, rbc, rbc[:], rbc, rbc[:], AF.Exp, scale=-0.5)
    for j in range(nch):
        vec(C, "scalar_tensor_tensor", dict(out=outT[:, j, :], in0=pbs[j][:, 0:T], scalar=C.smalls[:, gcol + j:gcol + j + 1],
                                            in1=rbc[:], op0=ALU.mult, op1=ALU.mult), [pbs[j], C.smalls, rbc], [outT])


def l1proj(C, T=512):
    P, A, I = C.P, C.A, C.I
    NB = T // 128
    NT = S // T
    m0 = A.mark()
    C.nbank = 7
    pe_filler(C, C.pb[7])
    w_in = load_w(C, "w_in1", I["w_in1"], 8, 0, 928)
    w_krB = load_w(C, "w_krB", I["w_krB"], 8, 0, 32)
    w_qn = load_w(C, "w_uq_nope", I["w_uq_nope"], 3, 0, 768)
    w_qA = load_w(C, "w_uq_rA", I["w_uq_rA"], 3, 0, 384)
    w_qB = load_w(C, "w_uq_rB", I["w_uq_rB"], 3, 0, 384)
    w_uk = load_w(C, "w_uk", I["w_uk"], 2, 0, 768)
    w_uv = load_w(C, "w_uv", I["w_uv"], 2, 0, 768)
    ones_f = A.alloc("ones_f", [128, 128], F32)
    vec(C, "memset", dict(ap=ones_f[:], constant=1.0), [], [ones_f])
    hT = A.alloc("hT", [128, 8, T], BF16)
    xin = [A.alloc("xin%d" % i, [128, 1024], F32) for i in range(2)]
    junk = A.alloc("junk", [128, 1024], BF16)
    xn = A.alloc("xn", [128, 1024], BF16)
    ss = [A.alloc("ss%d" % i, [128, 1], F32) for i in range(2)]
    t1 = [A.alloc("t1%d" % i, [128, 1], F32) for i in range(2)]
    rstd = [A.alloc("rstd%d" % i, [128, 1], F32) for i in range(2)]
    cosT = [A.alloc("cosT%d" % i, [128, T], F32) for i in range(2)]
    sinT = [A.alloc("sinT%d" % i, [128, T], F32) for i in range(2)]
    sq = [A.alloc("sq%d" % i, [128, T], F32) for i in range(2)]
    rbc = [A.alloc("rbc%d" % i, [128, T], F32) for i in range(2)]
    cqn = [A.alloc("cqn%d" % i, [128, 3, T], BF16) for i in range(2)]
    ckvn = [A.alloc("ckvn%d" % i, [128, 2, T], BF16) for i in range(2)]
    ta = [A.alloc("ta%d" % i, [128, T], F32) for i in range(2)]
    tb = [A.alloc("tb%d" % i, [128, T], F32) for i in range(2)]
    stg = [A.alloc("stg%d" % i, [128, T], BF16) for i in range(12)]
    qmT = [A.alloc("qmT%d" % i, [128, 2, T], BF16) for i in range(2)]
    ymT = [A.alloc("ymT%d" % i, [128, 8, T], BF16) for i in range(1)]
    eT = [A.alloc("eT%d" % i, [128, T], BF16) for i in range(4)]
    rec = [A.alloc("rec%d" % i, [128, T], F32) for i in range(2)]
    vtok = [A.alloc("vtok%d" % i, [128, 768], BF16) for i in range(4)]
    cnt = {"x": 0, "e": 0, "s": 0, "r": 0, "v": 0}

    def A1(t):
        for b in range(NB):
            k = cnt["x"] % 2
            cnt["x"] += 1
            r0 = t * T + b * 128
            P.dma("sync", xin[k], xin[k][:], C.X2, C.X2[r0:r0 + 128, :])
            norm_transpose(C, xin[k], xin[k][:], 16, hT, b * 128, (junk, ss[k], t1[k], rstd[k], xn))

    def stage(nm):
        b_ = stg[cnt["s"] % 12]
        cnt["s"] += 1
        return b_

    def rope(pA, pB, np_, out_ap, out_buf, cs_, sn_):
        k = cnt["r"] % 2
        cnt["r"] += 1
        vec(C, "tensor_tensor", dict(out=ta[k][0:np_, :], in0=pA[0:np_, 0:T], in1=cs_[0:np_, :], op=ALU.mult),
            [pA, cs_], [ta[k]])
        vec(C, "tensor_tensor", dict(out=tb[k][0:np_, :], in0=pB[0:np_, 0:T], in1=sn_[0:np_, :], op=ALU.mult),
            [pB, sn_], [tb[k]])
        vec(C, "tensor_tensor", dict(out=out_ap, in0=ta[k][0:np_, :], in1=tb[k][0:np_, :], op=ALU.add),
            [ta[k], tb[k]], [out_buf], eng="gpsimd")

    def X(t):
        cols = slice(t * T, (t + 1) * T)
        cs_, sn_ = cosT[t % 2], sinT[t % 2]
        cq_, ckv_, qm_ = cqn[t % 2], ckvn[t % 2], qmT[t % 2]
        P.dma("sync", cs_, cs_[:], C.ROPEC, C.ROPEC[:, cols])
        P.dma("sync", sn_, sn_[:], C.ROPES, C.ROPES[:, cols])
        pcq = []
        for j in range(3):
            pb = bank(C)
            for c in range(8):
                mm(C, pb, pb[:, 0:T], w_in, w_in[:, c, CQ0 + j * 128:CQ0 + (j + 1) * 128], hT, hT[:, c, :],
                   start=(c == 0), stop=(c == 7))
            pcq.append(pb)
        pkv = []
        for j in range(2):
            pb = bank(C)
            for c in range(8):
                mm(C, pb, pb[:, 0:T], w_in, w_in[:, c, CKV0 + j * 128:CKV0 + (j + 1) * 128], hT, hT[:, c, :],
                   start=(c == 0), stop=(c == 7))
            pkv.append(pb)
        feat_rmsnorm(C, pcq, 3, 0, cq_, T, sq, rbc[0], ones_f)
        pA = bank(C)
        for c in range(8):
            mm(C, pA, pA[0:32, 0:T], w_in, w_in[:, c, KR0:KR0 + 32], hT, hT[:, c, :], start=(c == 0), stop=(c == 7))
        pB = bank(C)
        for c in range(8):
            mm(C, pB, pB[0:32, 0:T], w_krB, w_krB[:, c, :], hT, hT[:, c, :], start=(c == 0), stop=(c == 7))
        feat_rmsnorm(C, pkv, 2, 3, ckv_, T, sq, rbc[1], ones_f)
        st = stage("kr")
        rope(pA, pB, 32, st[0:32, :], st, cs_, sn_)
        P.dma("sync", C.KR, C.KR[:, cols], st, st[0:32, :])
        yield
        for pair in range(2):
            pb = bank(C)
            for c in range(8):
                mm(C, pb, pb[:, 0:T], w_in, w_in[:, c, QM1 + pair * 128:QM1 + (pair + 1) * 128], hT, hT[:, c, :],
                   start=(c == 0), stop=(c == 7))
            act(C, qm_, qm_[:, pair, :], pb, pb[:, 0:T], AF.Copy)
        yield
        if t + 1 < NT:
            for _ in A1g(t + 1):
                yield

    def A1g(t):
        for b in range(NB):
            k = cnt["x"] % 2
            cnt["x"] += 1
            r0 = t * T + b * 128
            P.dma("sync", xin[k], xin[k][:], C.X2, C.X2[r0:r0 + 128, :])
            norm_transpose(C, xin[k], xin[k][:], 16, hT, b * 128, (junk, ss[k], t1[k], rstd[k], xn))
            yield

    def Y(t):
        cols = slice(t * T, (t + 1) * T)
        cs_, sn_ = cosT[t % 2], sinT[t % 2]
        cq_, ckv_, qm_ = cqn[t % 2], ckvn[t % 2], qmT[t % 2]
        ym = ymT[0]
        mem_attn(C, qm_, ym, T, eT, rec, cnt)
        for pair in range(2):
            P.dma("gpsimd", C.YT, C.YT[768 + pair * 128:768 + (pair + 1) * 128, cols], ym, ym[:, 6 + pair, :])
        yield
        for g in range(6):
            pb = bank(C)
            for j in range(3):
                mm(C, pb, pb[:, 0:T], w_qn, w_qn[:, j, g * 128:(g + 1) * 128], cq_, cq_[:, j, :], start=(j == 0), stop=(j == 2))
            st = stage("qn")
            act(C, st, st[:], pb, pb[:, 0:T], AF.Copy)
            for i in range(2):
                P.dma("gpsimd", C.QT, C.QT[2 * g + i, 0:64, cols], st, st[64 * i:64 * i + 64, :])
            if g % 2 == 1:
                yield
        for g in range(3):
            pA = bank(C)
            for j in range(3):
                mm(C, pA, pA[:, 0:T], w_qA, w_qA[:, j, g * 128:(g + 1) * 128], cq_, cq_[:, j, :], start=(j == 0), stop=(j == 2))
            pB = bank(C)
            for j in range(3):
                mm(C, pB, pB[:, 0:T], w_qB, w_qB[:, j, g * 128:(g + 1) * 128], cq_, cq_[:, j, :], start=(j == 0), stop=(j == 2))
            st = stage("qr")
            rope(pA, pB, 128, st[:], st, cs_, sn_)
            for i in range(4):
                P.dma("gpsimd", C.QT, C.QT[4 * g + i, 64:96, cols], st, st[32 * i:32 * i + 32, :])
            yield
        for g in range(6):
            pb = bank(C)
            for j in range(2):
                mm(C, pb, pb[:, 0:T], w_uk, w_uk[:, j, g * 128:(g + 1) * 128], ckv_, ckv_[:, j, :], start=(j == 0), stop=(j == 1))
            st = stage("kn")
            act(C, st, st[:], pb, pb[:, 0:T], AF.Copy)
            for i in range(2):
                P.dma("gpsimd", C.KT, C.KT[2 * g + i, :, cols], st, st[64 * i:64 * i + 64, :])
            if g % 2 == 1:
                yield
        for b in range(NB):
            vk = vtok[cnt["v"] % 4]
            cnt["v"] += 1
            for half in range(2):
                pb = bank(C)
                for j in range(2):
                    mm(C, pb, pb[:, 0:384], ckv_, ckv_[:, j, b * 128:(b + 1) * 128], w_uv,
                       w_uv[:, j, half * 384:(half + 1) * 384], start=(j == 0), stop=(j == 1))
                act(C, vk, vk[:, half * 384:(half + 1) * 384], pb, pb[:, 0:384], AF.Copy)
            r0 = t * T + b * 128
            P.dma("gpsimd", C.VV, C.VV[r0:r0 + 128, :], vk, vk[:])
            if b % 2 == 1:
                yield

    def merge(gens):
        alive = list(gens)
        while alive:
            for g_ in list(alive):
                try:
                    next(g_)
                except StopIteration:
                    alive.remove(g_)

    for _ in A1g(0):
        pass
    for t in range(NT + 1):
        gens = []
        if t < NT:
            gens.append(X(t))
        if t >= 1:
            gens.append(Y(t - 1))
        merge(gens)
    C.nbank = 8
    barrier(P)
    pe_filler(C, None, on=False)
    A.release(m0)


def l1attn(C, T=512):
    P, A, I = C.P, C.A, C.I
    NQ = S // T
    m0 = A.mark()
    pe_filler(C, C.pb[2])
    HB = []
    for i in range(2):
        d = Ctx()
        d.K = A.alloc("Kh%d" % i, [96, S], BF16)
        d.Q = A.alloc("Qh%d" % i, [96, S], BF16)
        d.V = A.alloc("Vh%d" % i, [128, NBLK, 65], BF16)
        vec(C, "memset", dict(ap=d.V[:], constant=1.0), [], [d.V])
        HB.append(d)
    onesrow = A.alloc("onesrow", [128, 64], F32)
    vec(C, "memset", dict(ap=onesrow[:], constant=1.0), [], [onesrow])
    eT = [A.alloc("eT%d" % i, [128, 2 * T], BF16) for i in range(4)]
    rrow = [A.alloc("rrow%d" % i, [128, T], F32) for i in range(5)]
    nsb = [A.alloc("nsb%d" % i, [64, T], F32) for i in range(5)]
    yh = [A.alloc("yh%d" % i, [64, T], BF16) for i in range(5)]
    scale = 96.0 ** -0.5
    ne = 0
    nq = 0
    pair_banks = [C.pbp[2], C.pbp[3]]
    npsc = 0

    def load_head(hh):
        d = HB[hh % 2]
        P.dma("sync", d.K, d.K[0:64, :], C.KT, C.KT[hh, :, :])
        P.dma("sync", d.K, d.K[64:96, :], C.KR, C.KR[:, :])
        P.dma("sync", d.Q, d.Q[:, :], C.QT, C.QT[hh, :, :])
        vsrc = C.VV[:, hh * 64:(hh + 1) * 64].rearrange("(b p) d -> p b d", p=128)
        for q4 in range(4):
            P.dma("sync", d.V, d.V[:, q4 * 8:(q4 + 1) * 8, 0:64], C.VV, vsrc[:, q4 * 8:(q4 + 1) * 8, :])

    rhl = [A.alloc("rhl%d" % i, [128, 2, T], BF16) for i in range(5)]
    onesb = A.alloc("onesb", [128, 64], BF16)
    vec(C, "memset", dict(ap=onesb[:], constant=1.0), [], [onesb])
    LAG = 3

    items = []
    for hh in range(12):
        for qt in range(NQ):
            nkb = 4 * qt + 4
            for kb in range(nkb):
                items.append((hh, qt, kb, nkb))

    state = {"npsc": 0, "ne": 0, "pe_ns": 0.0}

    def frontp(ia, ib):
        hh, qt, kba, nkb = ia
        kbb = ib[2]
        d = HB[hh % 2]
        psc = pair_banks[state["npsc"] % 2]
        state["npsc"] += 1
        e = eT[state["ne"] % 4]
        state["ne"] += 1
        c0s = []
        for half, kb in ((0, kba), (1, kbb)):
            j = kb - 4 * qt
            c0 = 128 * j if j > 0 else 0
            c0s.append(c0)
            mm(C, psc, psc[:, half * T + c0:(half + 1) * T], d.K, d.K[:, kb * 128:(kb + 1) * 128],
               d.Q, d.Q[:, qt * T + c0:(qt + 1) * T])
        lo = c0s[0]
        act(C, e, e[:, lo:2 * T], psc, psc[:, lo:2 * T], AF.Exp, scale=scale)
        for half, kb in ((0, kba), (1, kbb)):
            j = kb - 4 * qt
            if j >= 0:
                a0 = half * T + c0s[half]
                vec(C, "tensor_tensor", dict(out=e[:, a0:a0 + 128], in0=e[:, a0:a0 + 128], in1=C.tri_b[:], op=ALU.mult),
                    [e, C.tri_b], [e], eng="gpsimd")
        return [(c0s[0], e, 0), (c0s[1], e, 1)]

    def back(it, fr):
        hh, qt, kb, nkb = it
        c0, e, half = fr
        d = HB[hh % 2]
        qi = hh * NQ + qt
        pa = C.pb[qi % 2]
        k2 = qi % 5
        if qt == 0 and kb == 0 and hh + 1 < 12:
            load_head(hh + 1)
        mm(C, pa, pa[0:65, c0:T], d.V, d.V[:, kb, :], e, e[:, half * T + c0:(half + 1) * T], start=(kb == 0), stop=(kb == nkb - 1))
        state["pe_ns"] += 2 * (T - c0) / 2.4
        if kb == nkb - 1:
            rr = rrow[k2]
            vec(C, "tensor_copy", dict(out=rr[64:65, :], in_=pa[64:65, 0:T]), [pa], [rr])
            vec(C, "tensor_copy", dict(out=nsb[k2][:], in_=pa[0:64, 0:T]), [pa], [nsb[k2]])
            vec(C, "reciprocal", dict(out=rr[64:65, :], in_=rr[64:65, :]), [rr], [rr])
            vec(C, "tensor_copy", dict(out=rhl[k2][64:65, 0, :], in_=rr[64:65, :]), [rr], [rhl[k2]])
            vec(C, "tensor_tensor", dict(out=rhl[k2][64:65, 1, :], in0=rr[64:65, :], in1=rhl[k2][64:65, 0, :],
                                         op=ALU.subtract), [rr, rhl[k2]], [rhl[k2]])

            def norm2(k2=k2, hh=hh, qt=qt):
                pbc = C.pb[3]
                mm(C, pbc, pbc[0:64, 0:T], onesb, onesb[64:65, :], rhl[k2], rhl[k2][64:65, 0, :], start=True, stop=False)
                mm(C, pbc, pbc[0:64, 0:T], onesb, onesb[64:65, :], rhl[k2], rhl[k2][64:65, 1, :], start=False, stop=True)
                vec(C, "tensor_tensor", dict(out=yh[k2][:], in0=nsb[k2][:], in1=pbc[0:64, 0:T], op=ALU.mult),
                    [nsb[k2], pbc], [yh[k2]])
                P.dma("sync", C.YT, C.YT[hh * 64:(hh + 1) * 64, qt * T:(qt + 1) * T], yh[k2], yh[k2][:])
            return norm2
        return None

    load_head(0)
    pend = []
    LAGP = 2
    deferred = []

    def do_back(a, b):
        n2 = back(a, b)
        if n2 is not None:
            deferred.append([state["pe_ns"] + 8000.0, n2])
        while deferred and (deferred[0][0] <= state["pe_ns"] or len(deferred) >= 4):
            deferred.pop(0)[1]()

    for i in range(0, len(items), 2):
        ia, ib = items[i], items[i + 1]
        fr = frontp(ia, ib)
        pend.append(((ia, fr[0]), (ib, fr[1])))
        if len(pend) > LAGP:
            pa_, pb_ = pend.pop(0)
            do_back(*pa_)
            do_back(*pb_)
    while pend:
        pa_, pb_ = pend.pop(0)
        do_back(*pa_)
        do_back(*pb_)
    for dfr in deferred:
        dfr[1]()
    barrier(P)
    pe_filler(C, None, on=False)
    A.release(m0)


def l1out(C, T=512):
    P, A, I = C.P, C.A, C.I
    NB = T // 128
    NT = S // T
    m0 = A.mark()
    C.nbank = 7
    pe_filler(C, C.pb[7])
    w_out = load_w(C, "w_out1", I["w_out1"], 8, 0, 1024)
    yT = [A.alloc("yT%d" % i, [128, 8, T], BF16) for i in range(3)]
    xres = [A.alloc("xres%d" % i, [128, 1024], F32) for i in range(8)]
    xo = [A.alloc("xo%d" % i, [128, 1024], F32) for i in range(4)]
    ysrc = C.YT.t.rearrange("(c p) t -> p c t", p=128)

    def loads(t):
        y = yT[t % 3]
        P.dma("sync", y, y[:], C.YT, ysrc[:, :, t * T:(t + 1) * T])
        for b in range(NB):
            blk = t * NB + b
            P.dma("sync", xres[blk % 8], xres[blk % 8][:], C.X2, C.X2[blk * 128:(blk + 1) * 128, :])

    loads(0)
    for t in range(NT):
        if t + 1 < NT:
            loads(t + 1)
        y = yT[t % 3]
        for b in range(NB):
            blk = t * NB + b
            k = blk % 4
            r0 = blk * 128
            cs = slice(b * 128, (b + 1) * 128)
            for half in range(2):
                pb = bank(C)
                for c in range(8):
                    mm(C, pb, pb[:, 0:512], y, y[:, c, cs], w_out, w_out[:, c, half * 512:(half + 1) * 512],
                       start=(c == 0), stop=(c == 7))
                vec(C, "tensor_tensor", dict(out=xo[k][:, half * 512:(half + 1) * 512], in0=pb[:, 0:512],
                                             in1=xres[blk % 8][:, half * 512:(half + 1) * 512], op=ALU.add),
                    [pb, xres[blk % 8]], [xo[k]])
            P.dma("gpsimd", C.X3, C.X3[r0:r0 + 128, :], xo[k], xo[k][:])
    C.nbank = 8
    barrier(P)
    pe_filler(C, None, on=False)
    A.release(m0)

import numpy as np

ARENA_BYTES = 200 * 1024

INPUT_SPECS = [
    ("x", [S, D], F32), ("mem", [256, D], F32), ("pos", [1, S], I32),
    ("w_mem_kv", [1024, 512], F32), ("w_in0", [1024, 2568], F32), ("w_out0", [1024, 1024], F32),
    ("w_ff1_0", [1024, 4096], F32), ("w_ff2_0", [4096, 1024], F32),
    ("w_in1", [1024, 928], F32), ("w_krB", [1024, 32], F32),
    ("w_uq_nope", [384, 768], F32), ("w_uq_rA", [384, 384], F32), ("w_uq_rB", [384, 384], F32),
    ("w_uk", [256, 768], F32), ("w_uv", [256, 768], F32), ("w_out1", [1024, 1024], F32),
    ("w_ff1_1", [1024, 4096], F32), ("w_ff2_1", [4096, 1024], F32),
    ("gains", [128, 40], F32), ("smalls", [128, 8], F32), ("convT", [96, 32], F32), ("bif", [4, 2], F32),
    ("whn", [1, 768], F32), ("gfinal", [1, 1024], F32), ("ident", [128, 128], F32), ("tri", [128, 128], F32),
]


def build(upto=99, dbg=(), only=None):
    nc = bass.Bass("TRN2", target_bir_lowering=False)
    import os
    P = Prog(nc, strict_same=tuple(x for x in os.environ.get('STRICT', 'vector,scalar,gpsimd').split(',') if x))
    C = Ctx()
    C.P = P
    C.I = {}
    for name, shape, dt in INPUT_SPECS:
        C.I[name] = P.dram(name, shape, dt, kind="ExternalInput")
    C.out = P.dram("out", [S, D], F32, kind="ExternalOutput")
    C.out.multi = True

    def scratch(name, shape, dt):
        b = P.dram(name, shape, dt, kind=("ExternalOutput" if name in dbg else "Internal"))
        b.multi = True
        return b

    C.ROPEC = scratch("ROPEC", [128, S], F32)
    C.ROPES = scratch("ROPES", [128, S], F32)
    C.X1 = scratch("X1", [S, D], F32)
    C.X2 = scratch("X2", [S, D], F32)
    C.X3 = scratch("X3", [S, D], F32)
    C.QT = scratch("QT", [12, 96, S], BF16)
    C.KT = scratch("KT", [12, 64, S], BF16)
    C.KR = scratch("KR", [32, S], BF16)
    C.VV = scratch("VV", [S, 768], BF16)
    C.YT = scratch("YT", [1024, S], BF16)
    C.A = Arena(P, ARENA_BYTES)
    C.pbp = [P.psum("pp%d" % j, [128, 1024], F32) for j in range(4)]
    C.pb = []
    for j in range(4):
        for h_ in range(2):
            C.pb.append(Buf(C.pbp[j].t[:, 512 * h_:512 * (h_ + 1)], "pb%d" % (2 * j + h_)))
    C.bi = 0
    on = (lambda i: upto >= i) if only is None else (lambda i: i in only)
    C.WB = {}
    C.bgcast = False
    phase0(C)
    if on(1):
        l0mix(C)
    if on(2):
        ffn(C, C.X1, C.X2, 8, C.I["w_ff1_0"], C.I["w_ff2_0"])
    if on(3):
        l1proj(C)
    pre = None
    if on(6):
        pre = (load_w(C, "W1p", C.I["w_ff1_1"], 8, 0, DFF), None)
    if on(4):
        l1attn(C)
    if on(5):
        l1out(C)
    if on(6):
        ffn(C, C.X3, C.out, 24, C.I["w_ff1_1"], C.I["w_ff2_1"], final=True, pre=pre)
    fin = [C.out] if upto >= 99 else []
    for nm in dbg:
        fin.append(getattr(C, nm))
    P.finish("sync", fin)
    P.emit()
    return nc, C


def prep_inputs(inp, b):
    f = lambda a: np.ascontiguousarray(a, dtype=np.float32)
    m = {}
    m["x"] = f(inp["x"][b])
    m["mem"] = f(inp["mem"][b])
    m["pos"] = np.ascontiguousarray(inp["positions"][b].reshape(1, S).astype(np.int32))
    for k in ("w_mem_kv", "w_in0", "w_out0", "w_ff1_0", "w_ff2_0", "w_in1", "w_out1", "w_ff1_1", "w_ff2_1"):
        m[k] = f(inp[k])
    w_in1 = inp["w_in1"]
    kr = w_in1[:, 640:672]
    m["w_krB"] = f(np.concatenate([kr[:, 16:32], kr[:, 0:16]], axis=1))
    wuq = inp["w_uq1"].reshape(384, 12, 96)
    m["w_uq_nope"] = f(wuq[:, :, 0:64].reshape(384, 768))
    m["w_uq_rA"] = f(wuq[:, :, 64:96].reshape(384, 384))
    m["w_uq_rB"] = f(np.concatenate([wuq[:, :, 80:96], wuq[:, :, 64:80]], axis=2).reshape(384, 384))
    wukv = inp["w_ukv1"].reshape(256, 12, 128)
    m["w_uk"] = f(wukv[:, :, 0:64].reshape(256, 768))
    m["w_uv"] = f(wukv[:, :, 64:128].reshape(256, 768))
    gains = np.zeros((128, 40), np.float32)
    for i, k in enumerate(("norm_mix0", "norm_ffn0", "norm_mix1", "norm_ffn1", "mem_norm")):
        gains[:, i * 8:(i + 1) * 8] = inp[k].reshape(8, 128).T
    m["gains"] = gains
    sm = np.zeros((128, 8), np.float32)
    sm[:, 0:3] = inp["w_qnorm1"].reshape(3, 128).T
    sm[:, 3:5] = inp["w_kvnorm1"].reshape(2, 128).T
    inv = (10000.0 ** (-np.arange(0, 32, 2, dtype=np.float32) / 32)).astype(np.float32)
    p = np.arange(128)
    sm[:, 5] = inv[p % 16]
    sm[:, 6] = np.where((p % 32) < 16, -1.0, 1.0)
    m["smalls"] = sm
    wc = inp["w_conv0"]
    m["convT"] = f(wc.reshape(4, 8, 96).transpose(2, 1, 0).reshape(96, 32))
    m["bif"] = f(np.stack([inp["b_igate0"], inp["b_fgate0"]], axis=1))
    m["whn"] = f(inp["w_hnorm0"].reshape(1, 768))
    m["gfinal"] = f(inp["final_norm"].reshape(1, 1024))
    m["ident"] = np.eye(128, dtype=np.float32)
    m["tri"] = np.triu(np.ones((128, 128), np.float32))
    return m

from concourse.bass_utils import run_bass_kernel_spmd

_NC = None


def kernel(**inputs):
    global _NC
    inp = {k: np.asarray(v) for k, v in inputs.items()}
    if _NC is None:
        _NC = build(upto=99)[0]
    nb = inp["x"].shape[0]
    in_maps = [prep_inputs(inp, b) for b in range(nb)]
    res = run_bass_kernel_spmd(_NC, in_maps, core_ids=list(range(nb)))
    return np.stack([np.asarray(r["out"], dtype=np.float32) for r in res.results], axis=0)
```
